# Optimizing a Trainium2 kernel written in Bass

```python
import jax, jax.numpy as jnp
from jax import lax
import numpy as np

D_MODEL = 2048
BATCH = 1
SEQ = 16384
DEPTH = 1
DEC_BATCH = 2
DEC_SEQ = 16384
PAST_LEN = 128

HEAD_DIM = 128
DIL_GROUPS = ((128, 1), (512, 4), (2048, 16))
N_DIL_GROUPS = 3
DIL_HEADS = 4
DIL_QKV_WIDTH = N_DIL_GROUPS * DIL_HEADS * HEAD_DIM
DIL_OUT_WIDTH = DIL_HEADS * HEAD_DIM
BAND_BLOCK = 64
GQA_Q_HEADS = 12
GQA_KV_HEADS = 4
GQA_GROUP = GQA_Q_HEADS // GQA_KV_HEADS
GQA_Q_WIDTH = GQA_Q_HEADS * HEAD_DIM
GQA_KV_WIDTH = GQA_KV_HEADS * HEAD_DIM
GQA_OUT_WIDTH = GQA_Q_WIDTH
Q_BLOCK = 128
GRID_W = 64
ROPE_THETA = 10000.0
D_FF = 4 * D_MODEL
IN_SPLITS = (DIL_QKV_WIDTH, DIL_QKV_WIDTH, DIL_QKV_WIDTH,
             GQA_Q_WIDTH, GQA_KV_WIDTH, GQA_KV_WIDTH, D_MODEL, D_MODEL)
IN_WIDTH = sum(IN_SPLITS)
BRANCH_WIDTH = DIL_OUT_WIDTH + GQA_OUT_WIDTH
RMS_EPS = 1e-6
NEG_INF = -1e30

kernel_name = "hybrid_dilated_gqa_encoder"


def rms_norm(x, g):
    xf = x.astype(jnp.float32)
    y = xf * lax.rsqrt(jnp.mean(xf * xf, axis=-1, keepdims=True) + RMS_EPS)
    return (y * g.astype(jnp.float32)).astype(x.dtype)


def alibi_slopes():
    n = N_DIL_GROUPS * DIL_HEADS
    return 2.0 ** (-8.0 * jnp.arange(1, n + 1, dtype=jnp.float32) / n)


def dilated_band_attention(q, k, v, window, dilation, slopes):
    B, S, H, Dh = q.shape
    n_side = window // (2 * dilation)
    L = S // dilation
    nb = -(-L // BAND_BLOCK)
    Lp = nb * BAND_BLOCK

    def to_sub(a):
        return a.reshape(B, L, dilation, H, Dh).transpose(0, 2, 1, 3, 4)

    qs = jnp.pad(to_sub(q), ((0, 0), (0, 0), (0, Lp - L), (0, 0), (0, 0)))
    qs = qs.reshape(B, dilation, nb, BAND_BLOCK, H, Dh)

    def key_blocks(a):
        ap = jnp.pad(to_sub(a), ((0, 0), (0, 0), (BAND_BLOCK, Lp - L + BAND_BLOCK), (0, 0), (0, 0)))
        ap = ap.reshape(B, dilation, nb + 2, BAND_BLOCK, H, Dh)
        return jnp.concatenate([ap[:, :, :-2], ap[:, :, 1:-1], ap[:, :, 2:]], axis=3)

    kb = key_blocks(k)
    vb = key_blocks(v)
    s = jnp.einsum('brnqhd,brnkhd->brnhqk', qs, kb,
                   preferred_element_type=jnp.float32) * (HEAD_DIM ** -0.5)
    rel = (jnp.arange(3 * BAND_BLOCK)[None, :] - BAND_BLOCK
           - jnp.arange(BAND_BLOCK)[:, None])
    key_idx = (jnp.arange(nb)[:, None] * BAND_BLOCK - BAND_BLOCK
               + jnp.arange(3 * BAND_BLOCK)[None, :])
    mask = (jnp.abs(rel) <= n_side)[None] & ((key_idx >= 0) & (key_idx < L))[:, None, :]
    dist = (dilation * jnp.abs(rel)).astype(jnp.float32)
    s = s - slopes[:, None, None] * dist
    s = jnp.where(mask[:, None], s, NEG_INF)
    lse = jax.nn.logsumexp(s, axis=-1)
    p = jnp.exp(s - lse[..., None]).astype(v.dtype)
    o = jnp.einsum('brnhqk,brnkhd->brnqhd', p, vb)
    o = o.reshape(B, dilation, Lp, H, Dh)[:, :, :L].transpose(0, 2, 1, 3, 4).reshape(B, S, H, Dh)
    lse = lse.transpose(0, 1, 2, 4, 3).reshape(B, dilation, Lp, H)[:, :, :L]
    lse = lse.transpose(0, 2, 1, 3).reshape(B, S, H)
    return o, lse


def dilated_mixer(q, k, v):
    B, S, _ = q.shape
    shp = (B, S, N_DIL_GROUPS, DIL_HEADS, HEAD_DIM)
    q, k, v = q.reshape(shp), k.reshape(shp), v.reshape(shp)
    slopes = alibi_slopes().reshape(N_DIL_GROUPS, DIL_HEADS)
    outs, lses = [], []
    for g, (w, d) in enumerate(DIL_GROUPS):
        o, l = dilated_band_attention(q[:, :, g], k[:, :, g], v[:, :, g], w, d, slopes[g])
        outs.append(o)
        lses.append(l)
    wts = jax.nn.softmax(jnp.stack(lses, axis=0), axis=0)
    y = sum(wts[g][..., None] * outs[g].astype(jnp.float32) for g in range(N_DIL_GROUPS))
    return y.reshape(B, S, DIL_OUT_WIDTH).astype(q.dtype)


def axial_rope_tables(S):
    rows = S // GRID_W
    row = jnp.broadcast_to(jnp.arange(rows)[:, None], (rows, GRID_W)).reshape(S).astype(jnp.float32)
    col = jnp.broadcast_to(jnp.arange(GRID_W)[None, :], (rows, GRID_W)).reshape(S).astype(jnp.float32)
    half = HEAD_DIM // 2
    inv_freq = ROPE_THETA ** (-jnp.arange(0, half, 2, dtype=jnp.float32) / half)
    ang = jnp.stack([row[:, None] * inv_freq, col[:, None] * inv_freq], axis=1)
    return jnp.cos(ang), jnp.sin(ang)


def apply_axial_rope(x, cos, sin):
    B, S, H, _ = x.shape
    xf = x.astype(jnp.float32).reshape(B, S, H, 2, 2, HEAD_DIM // 4)
    x1, x2 = xf[..., 0, :], xf[..., 1, :]
    c = cos[None, :, None]
    s = sin[None, :, None]
    out = jnp.stack([x1 * c - x2 * s, x2 * c + x1 * s], axis=-2)
    return out.reshape(B, S, H, HEAD_DIM).astype(x.dtype)


def gqa_mixer(q, k, v, g_q, g_k):
    B, S, _ = q.shape
    q = rms_norm(q.reshape(B, S, GQA_Q_HEADS, HEAD_DIM), g_q)
    k = rms_norm(k.reshape(B, S, GQA_KV_HEADS, HEAD_DIM), g_k)
    v = v.reshape(B, S, GQA_KV_HEADS, HEAD_DIM)
    cos, sin = axial_rope_tables(S)
    q = apply_axial_rope(q, cos, sin)
    k = apply_axial_rope(k, cos, sin)
    nq = S // Q_BLOCK
    qb = q.reshape(B, nq, Q_BLOCK, GQA_KV_HEADS, GQA_GROUP, HEAD_DIM).transpose(1, 0, 2, 3, 4, 5)
    scale = HEAD_DIM ** -0.5

    def attend(qblk):
        s = jnp.einsum('bqhgd,bkhd->bhgqk', qblk, k, preferred_element_type=jnp.float32) * scale
        p = jax.nn.softmax(s, axis=-1).astype(v.dtype)
        return jnp.einsum('bhgqk,bkhd->bqhgd', p, v)

    o = lax.map(attend, qb)
    return o.transpose(1, 0, 2, 3, 4, 5).reshape(B, S, GQA_OUT_WIDTH)


def encoder_layer(x, g_mix, w_in, g_q, g_k, w_branch, w_out, g_mlp, w_ff1, w_ff2):
    h = rms_norm(x, g_mix)
    proj = h @ w_in
    idx = np.cumsum(IN_SPLITS)[:-1].tolist()
    aq, ak, av, bq, bk, bv, ga, gb = jnp.split(proj, idx, axis=-1)
    y_a = dilated_mixer(aq, ak, av)
    y_b = gqa_mixer(bq, bk, bv, g_q, g_k)
    o_a = y_a @ w_branch[:DIL_OUT_WIDTH]
    o_b = y_b @ w_branch[DIL_OUT_WIDTH:]
    merged = jax.nn.sigmoid(ga) * o_a + jax.nn.sigmoid(gb) * o_b
    x = x + merged @ w_out
    h2 = rms_norm(x, g_mlp)
    u = jax.nn.relu(h2 @ w_ff1)
    return x + (u * u) @ w_ff2


def trunk(x, g_mix, w_in, g_q, g_k, w_branch, w_out, g_mlp, w_ff1, w_ff2, g_final):
    for l in range(DEPTH):
        x = encoder_layer(x, g_mix[l], w_in[l], g_q[l], g_k[l], w_branch[l], w_out[l],
                          g_mlp[l], w_ff1[l], w_ff2[l])
    return rms_norm(x, g_final)


def setup_inputs(seed: int = 0) -> dict:
    key = jax.random.key(seed)
    ks = jax.random.split(key, 13)
    f32 = jnp.float32
    nrm = lambda k, shp: jax.random.normal(k, shp, f32)
    branch_scale = jnp.concatenate([jnp.full((DIL_OUT_WIDTH,), DIL_OUT_WIDTH ** -0.5, f32),
                                    jnp.full((GQA_OUT_WIDTH,), GQA_OUT_WIDTH ** -0.5, f32)])[:, None]
    return {
        "x_prompt": nrm(ks[0], (BATCH, SEQ, D_MODEL)),
        "x_sample": nrm(ks[1], (DEC_BATCH, DEC_SEQ, D_MODEL)),
        "g_mix": 1.0 + 0.02 * nrm(ks[2], (DEPTH, D_MODEL)),
        "w_in": nrm(ks[3], (DEPTH, D_MODEL, IN_WIDTH)) * D_MODEL ** -0.5,
        "g_q": 1.0 + 0.02 * nrm(ks[4], (DEPTH, HEAD_DIM)),
        "g_k": 1.0 + 0.02 * nrm(ks[5], (DEPTH, HEAD_DIM)),
        "w_branch": nrm(ks[6], (DEPTH, BRANCH_WIDTH, D_MODEL)) * branch_scale,
        "w_out": nrm(ks[7], (DEPTH, D_MODEL, D_MODEL)) * D_MODEL ** -0.5,
        "g_mlp": 1.0 + 0.02 * nrm(ks[8], (DEPTH, D_MODEL)),
        "w_ff1": nrm(ks[9], (DEPTH, D_MODEL, D_FF)) * D_MODEL ** -0.5,
        "w_ff2": nrm(ks[10], (DEPTH, D_FF, D_MODEL)) * D_FF ** -0.5,
        "g_final": 1.0 + 0.02 * nrm(ks[11], (D_MODEL,)),
    }


def reference(x_prompt, x_sample, g_mix, w_in, g_q, g_k, w_branch, w_out, g_mlp, w_ff1, w_ff2, g_final):
    y_prompt = trunk(x_prompt, g_mix, w_in, g_q, g_k, w_branch, w_out, g_mlp, w_ff1, w_ff2, g_final)
    y_sample = trunk(x_sample, g_mix, w_in, g_q, g_k, w_branch, w_out, g_mlp, w_ff1, w_ff2, g_final)
    return (y_prompt, y_sample)
```

```python
import contextlib
import math
import numpy as np
import concourse.bass as bass
import concourse.mybir as mybir
from concourse.bass_utils import run_bass_kernel_spmd

F32 = mybir.dt.float32
BF16 = mybir.dt.bfloat16
AF = mybir.ActivationFunctionType
ALU = mybir.AluOpType

SEQ = 16384
DM = 2048
NCORE = 8
CH = 2048
NSEQ = 3
DIL = (1, 4, 16)
COL_AQ, COL_AK, COL_AV, COL_BQ, COL_BK, COL_GA, COL_GB = 0, 1536, 3072, 4608, 6144, 7168, 9216
IN_W = 11264
DFF = 8192
EPS = 1e-6
NEG = -30000.0
SCALE = 128 ** -0.5
KT_BASE = (0, 17, 37)
NKT = 69

COMPUTE = ("pe", "act", "dve", "pool")
EPOCH = 12000


class Buf:
    __slots__ = ("name", "w", "r", "excl")

    def __init__(self, name="", excl=False):
        self.name = name
        self.w = None
        self.r = []
        self.excl = excl


class Op:
    __slots__ = ("eng", "emit", "deps", "marked", "dma", "sem", "val", "slot_prev")

    def __init__(self, eng, emit, dma):
        self.eng = eng
        self.emit = emit
        self.deps = []
        self.marked = False
        self.dma = dma
        self.sem = None
        self.val = 0
        self.slot_prev = None


class Prog:
    def __init__(self, nc, dma_ring=16):
        self.nc = nc
        self.ops = {e: [] for e in ("pe", "act", "dve", "pool", "sp")}
        self.dma_ring = dma_ring
        self.dmas = []

    def op(self, eng, emit, reads=(), writes=(), dma=False):
        o = Op(eng, emit, dma)
        deps = {}
        writes = list(writes) + [b for b in reads if b.excl]
        reads = [b for b in reads if not b.excl]
        for b in reads:
            if b.w is not None:
                deps[id(b.w)] = (b.w, "raw")
        for b in writes:
            if b.w is not None and id(b.w) not in deps:
                deps[id(b.w)] = (b.w, "waw")
            for r in b.r:
                if id(r) not in deps:
                    deps[id(r)] = (r, "war")
        for d, kind in deps.values():
            if d is o:
                continue
            if (not dma) and (not d.dma) and d.eng == eng:
                if eng == "pe" or kind != "raw":
                    continue
            o.deps.append(d)
            d.marked = True
        for b in reads:
            b.r.append(o)
        for b in writes:
            b.w = o
            b.r = []
        self.ops[eng].append(o)
        if dma:
            self.dmas.append(o)
        return o

    def dma(self, q, out, in_, reads=(), writes=(), **kw):
        return self.op(q, lambda e: e.dma_start(out=out, in_=in_, **kw), reads, writes, dma=True)

    def barrier(self):
        lasts = []
        for e in COMPUTE:
            for o in reversed(self.ops[e]):
                if not o.dma and o.emit is not None:
                    lasts.append(o)
                    break
        pend = lasts + self.dmas
        self.dmas = []
        for e in ("act", "dve", "pool", "sp"):
            o = Op(e, None, False)
            for d in pend:
                if d.eng == e and not d.dma:
                    continue
                o.deps.append(d)
                d.marked = True
            self.ops[e].append(o)

    def emit_all(self, final_ops):
        nc = self.nc
        with contextlib.ExitStack() as st:
            eng_sems = {}
            for e in COMPUTE:
                cnt = 0
                ep = 0
                sems = [st.enter_context(nc.semaphore(f"s_{e}_0"))]
                for o in self.ops[e]:
                    if o.dma or not o.marked or o.emit is None:
                        continue
                    if cnt >= EPOCH:
                        ep += 1
                        cnt = 0
                        sems.append(st.enter_context(nc.semaphore(f"s_{e}_{ep}")))
                    cnt += 1
                    o.sem = (e, ep)
                    o.val = cnt
                eng_sems[e] = sems
            dma_sems = {}
            for q in ("sp", "act", "pool"):
                dl = [o for o in self.ops[q] if o.dma]
                if not dl:
                    continue
                nslot = min(self.dma_ring, len(dl))
                sems = [st.enter_context(nc.semaphore(f"d_{q}_{i}")) for i in range(nslot)]
                last = [None] * nslot
                vals = [0] * nslot
                for i, o in enumerate(dl):
                    s = i % nslot
                    vals[s] += 16
                    o.sem = (q + "_dma", s)
                    o.val = vals[s]
                    o.slot_prev = last[s]
                    last[s] = o
                dma_sems[q] = sems

            def sem_of(o):
                k, i = o.sem
                if k.endswith("_dma"):
                    return dma_sems[k[:-4]][i]
                return eng_sems[k][i]

            block = st.enter_context(nc.Block())
            handles = {"pe": block.tensor, "act": block.scalar, "dve": block.vector,
                       "pool": block.gpsimd, "sp": block.sync}

            def make(ename):
                ops = self.ops[ename]
                finals = final_ops if ename == "sp" else []

                def body(eng):
                    waited = {}

                    def wait_for(d):
                        k, i = d.sem
                        if k.endswith("_dma"):
                            key = (k, i)
                            if waited.get(key, 0) >= d.val:
                                return
                            waited[key] = d.val
                        else:
                            cur = waited.get(k, (-1, 0))
                            if cur >= (i, d.val):
                                return
                            waited[k] = (i, d.val)
                        eng.wait_ge(sem_of(d), d.val)

                    for o in ops:
                        if o.dma and o.slot_prev is not None:
                            wait_for(o.slot_prev)
                        for d in o.deps:
                            wait_for(d)
                        if o.emit is None:
                            continue
                        ins = o.emit(eng)
                        if o.dma:
                            ins.then_inc(sem_of(o), 16)
                        elif o.marked:
                            ins.then_inc(sem_of(o), 1)
                    for d in finals:
                        wait_for(d)
                return body

            for ename in ("sp", "act", "dve", "pool", "pe"):
                handles[ename](make(ename))


class T:
    __slots__ = ("ap", "b")

    def __init__(self, ap, name=""):
        self.ap = ap
        self.b = Buf(name)


class Arena:
    def __init__(self, base_ap, nwords):
        self.base = base_ap
        self.n = nwords
        self.top = 0

    def mark(self):
        return self.top

    def reset(self, m):
        self.top = m

    def take(self, name, free_shape, dtype):
        esz = 4 if dtype == F32 else 2
        nel = int(np.prod(free_shape))
        nw = (nel * esz + 3) // 4
        nw = (nw + 7) // 8 * 8
        off = self.top
        self.top += nw
        assert self.top <= self.n, f"arena overflow at {name}: {self.top*4} > {self.n*4}"
        v = self.base[:, off:off + (nel * esz) // 4]
        if dtype != F32:
            v = v.bitcast(dtype)
        if len(free_shape) > 1:
            names = [f"a{i}" for i in range(len(free_shape))]
            pat = "p (" + " ".join(names) + ") -> p " + " ".join(names)
            kw = {n: int(s) for n, s in zip(names[1:], free_shape[1:])}
            v = v.rearrange(pat, **kw)
        return T(v, name)


def build(nseq=NSEQ, debug=False, phases="ABCDE", ntl=SEQ // 512):
    nc = bass.Bass("TRN2", target_bir_lowering=False)
    import os
    DBG = os.environ.get("K_DBG", "").split(",")
    ROPEN = int(os.environ.get("K_ROPEN", "99"))

    def din(name, shape):
        return nc.dram_tensor(name, shape, F32, kind="ExternalInput").ap()

    xr = din("xr", [nseq * SEQ, DM])
    w_in = din("w_in", [DM, IN_W])
    w_br = din("w_branch", [DM, DM])
    w_out = din("w_out", [DM, DM])
    w_ff1 = din("w_ff1", [DM, DFF])
    w_ff2 = din("w_ff2", [DFF, DM])
    g_mix = din("g_mix", [1, DM])
    g_mlp = din("g_mlp", [1, DM])
    g_fin = din("g_final", [1, DM])
    g_q = din("g_q", [1, 128])
    g_k = din("g_k", [1, 128])
    cosT = din("cosT", [128, SEQ])
    sinT = din("sinT", [128, SEQ])
    abt = din("abt", [128, 12 * 256])
    kval = din("kval", [128, 72])
    rmat = din("rmat", [128, 128])
    y = nc.dram_tensor("y", [nseq * CH, DM], F32, kind="ExternalOutput").ap()
    skind = "ExternalOutput" if debug else "Internal"

    def dscr(name, shape):
        return nc.dram_tensor(name, shape, BF16, kind=skind).ap()

    wb_in = dscr("wb_in", [DM, IN_W])
    wb_br = dscr("wb_br", [DM, DM])
    wb_out = dscr("wb_out", [DM, DM])
    wb_ff1 = dscr("wb_ff1", [DM, DFF])
    wb_ff2 = dscr("wb_ff2", [DFF, DM])
    KT = dscr("KT", [nseq * 4, 128, SEQ])
    VG = dscr("VG", [nseq * 4, 128, 128, 128])
    AQ = dscr("AQ", [nseq * 12, 128, CH])
    AK = dscr("AK", [nseq * 12, 128, 4096])
    AV = dscr("AV", [nseq * 12, 128, 4096])
    BQ = dscr("BQ", [nseq * 12, 128, CH])
    YT = dscr("YT", [nseq * 16, 128, CH])
    D_ = {n: Buf(n) for n in ["wb_in", "wb_br", "wb_out", "wb_ff1", "wb_ff2", "KT", "VG", "AQ", "AK", "AV", "BQ", "YT", "y"]}

    P = Prog(nc)
    st = contextlib.ExitStack()
    with st:
        NW = 52480
        arena_t = st.enter_context(nc.sbuf_tensor("arena", [128, NW], F32))
        psum_t = st.enter_context(nc.psum_tensor("psum", [128, 4096], F32))
        A = Arena(arena_t[:, :], NW)
        PB = [T(psum_t[:, 512 * i:512 * (i + 1)], f"pb{i}") for i in range(8)]
        for t_ in PB:
            t_.b.excl = True

        def pb_bf(i):
            return PB[i].ap.bitcast(BF16).rearrange("p (a b) -> p a b", b=128)

        ident = A.take("ident", (128,), BF16)
        ones = A.take("ones", (128,), BF16)
        rm = A.take("rm", (128,), BF16)
        epsc = A.take("eps", (1,), F32)
        gcol = A.take("gcol", (4,), F32)
        gmixc = A.take("gmixc", (16,), F32)
        gmlpc = A.take("gmlpc", (16,), F32)
        kv = A.take("kv", (72,), F32)
        pm = A.mark()

        tmpf = A.take("tmpf", (128,), F32)
        P.op("pool", lambda e: e.memset(tmpf.ap, 0.0), writes=[tmpf.b])
        P.op("pool", lambda e: e.affine_select(out=tmpf.ap, in_=tmpf.ap, pattern=[[-1, 128]], compare_op=ALU.not_equal,
                                               fill=1.0, base=0, channel_multiplier=1), reads=[tmpf.b], writes=[tmpf.b])
        P.op("dve", lambda e: e.tensor_copy(out=ident.ap, in_=tmpf.ap), reads=[tmpf.b], writes=[ident.b])
        P.op("dve", lambda e: e.memset(ones.ap, 1.0), writes=[ones.b])
        P.op("dve", lambda e: e.memset(epsc.ap, EPS), writes=[epsc.b])
        tmpr = A.take("tmpr", (128,), F32)
        P.dma("sp", tmpr.ap, rmat[:, :], writes=[tmpr.b])
        P.op("dve", lambda e: e.tensor_copy(out=rm.ap, in_=tmpr.ap), reads=[tmpr.b], writes=[rm.b])
        NCD = dict(allow_slow_non_contiguous=True)
        P.dma("sp", gcol.ap[:, 0:1], g_q.rearrange("o d -> d o"), writes=[gcol.b], **NCD)
        P.dma("sp", gcol.ap[:, 2:3], g_k.rearrange("o d -> d o"), writes=[gcol.b], **NCD)
        for (lo, src) in ((0, 32), (32, 0), (64, 96), (96, 64)):
            P.dma("sp", gcol.ap[lo:lo + 32, 1:2], g_q[:, src:src + 32].rearrange("o d -> d o"), writes=[gcol.b], **NCD)
            P.dma("sp", gcol.ap[lo:lo + 32, 3:4], g_k[:, src:src + 32].rearrange("o d -> d o"), writes=[gcol.b], **NCD)
        for c in range(16):
            P.dma("sp", gmixc.ap[:, c:c + 1], g_mix[:, c * 128:(c + 1) * 128].rearrange("o d -> d o"), writes=[gmixc.b], **NCD)
            P.dma("sp", gmlpc.ap[:, c:c + 1], g_mlp[:, c * 128:(c + 1) * 128].rearrange("o d -> d o"), writes=[gmlpc.b], **NCD)
        P.dma("sp", kv.ap, kval[:, :], writes=[kv.b])

        def cast_copy(dst, src, rows, cols, dbuf):
            for r0 in range(0, rows, 512):
                for c0 in range(0, cols, 1024):
                    P.dma("pool", dst[r0:r0 + 512, c0:c0 + 1024], src[r0:r0 + 512, c0:c0 + 1024], writes=[dbuf])
        def scaled_copy(dst, src, cols, gc, dbuf):
            XS = [A.take(f"xs{i}", (2048,), F32) for i in range(4)]
            HS = [A.take(f"hs{i}", (2048,), BF16) for i in range(4)]
            k = 0
            for c in range(16):
                for c0 in range(0, cols, 2048):
                    wd = min(2048, cols - c0)
                    xs, hs = XS[k % 4], HS[k % 4]
                    P.dma("sp", xs.ap[:, :wd], src[c * 128:(c + 1) * 128, c0:c0 + wd], writes=[xs.b])
                    if k % 2 == 0:
                        P.op("dve", lambda e, xs=xs, hs=hs, c=c, wd=wd: e.tensor_scalar(
                            out=hs.ap[:, :wd], in0=xs.ap[:, :wd], scalar1=gc.ap[:, c:c + 1], scalar2=None, op0=ALU.mult),
                            reads=[xs.b, gc.b], writes=[hs.b])
                    else:
                        P.op("act", lambda e, xs=xs, hs=hs, c=c, wd=wd: e.activation(
                            out=hs.ap[:, :wd], in_=xs.ap[:, :wd], func=AF.Identity, scale=gc.ap[:, c:c + 1]),
                            reads=[xs.b, gc.b], writes=[hs.b])
                    P.dma("sp", dst[c * 128:(c + 1) * 128, c0:c0 + wd], hs.ap[:, :wd], reads=[hs.b], writes=[dbuf])
                    k += 1

        m0 = A.mark()
        if "p" not in phases:
            scaled_copy(wb_in, w_in, IN_W, gmixc, D_["wb_in"])
            scaled_copy(wb_ff1, w_ff1, DFF, gmlpc, D_["wb_ff1"])
        if "c" not in phases:
            cast_copy(wb_br, w_br, DM, DM, D_["wb_br"])
            cast_copy(wb_out, w_out, DM, DM, D_["wb_out"])
            cast_copy(wb_ff2, w_ff2, DFF, DM, D_["wb_ff2"])
        P.barrier()
        A.reset(pm)

        def rms_front(load_src, XBs, HBs, HTt, stat, tp_banks, pfx):
            for j in range(4):
                xb, hb = XBs[j], HBs[j % len(HBs)]
                ss, sd, rs = stat[j]
                if load_src is not None:
                    P.dma("sp", xb.ap, load_src(j), writes=[xb.b])
                P.op("act", lambda e, xb=xb, hb=hb, ss=ss: e.activation(out=hb.ap, in_=xb.ap, func=AF.Square, accum_out=ss.ap),
                     reads=[xb.b], writes=[hb.b, ss.b])
                P.op("act", lambda e, ss=ss, sd=sd: e.activation(out=sd.ap, in_=ss.ap, func=AF.Sqrt, scale=1.0 / DM, bias=epsc.ap[:, 0:1]),
                     reads=[ss.b, epsc.b], writes=[sd.b])
                P.op("dve", lambda e, sd=sd, rs=rs: e.reciprocal(out=rs.ap, in_=sd.ap), reads=[sd.b], writes=[rs.b])
                P.op("dve", lambda e, xb=xb, hb=hb, rs=rs: e.tensor_scalar(out=hb.ap, in0=xb.ap, scalar1=rs.ap[:, 0:1], scalar2=None, op0=ALU.mult),
                     reads=[xb.b, rs.b], writes=[hb.b])
                for half in range(2):
                    bk = tp_banks[half]
                    for c in range(8):
                        cc = half * 8 + c
                        P.op("pe", lambda e, bk=bk, c=c, cc=cc, hb=hb: e.transpose(out=pb_bf(bk)[:, c, :], in_=hb.ap[:, cc * 128:(cc + 1) * 128], identity=ident.ap),
                             reads=[hb.b, ident.b], writes=[PB[bk].b])
                    dst = HTt.ap[:, half * 8:(half + 1) * 8, j * 128:(j + 1) * 128]
                    if half == 0:
                        P.op("dve", lambda e, bk=bk, dst=dst: e.tensor_copy(out=dst, in_=pb_bf(bk)), reads=[PB[bk].b], writes=[HTt.b])
                    else:
                        P.op("act", lambda e, bk=bk, dst=dst: e.activation(out=dst, in_=pb_bf(bk), func=AF.Copy), reads=[PB[bk].b], writes=[HTt.b])

        def mk_stats(n, pfx):
            res = []
            for i in range(n):
                res.append(tuple(A.take(f"{pfx}{i}_{k}", (1,), F32) for k in range(3)))
            return res

        for s in range(nseq):
            if "A" in phases:
                A.reset(pm)
                Wkv = A.take("Wkv", (16, 1024), BF16)
                XB = [A.take(f"XB{i}", (2048,), F32) for i in range(4)]
                HB = [A.take(f"HB{i}", (2048,), BF16) for i in range(2)]
                HT = [A.take(f"HT{i}", (16, 512), BF16) for i in range(2)]
                WR = [A.take(f"WR{i}", (16, 512), BF16) for i in range(2)]
                OST = [A.take(f"OST{i}", (4, 512), BF16) for i in range(3)]
                VST = [A.take(f"VST{i}", (4, 512), BF16) for i in range(2)]
                CS = [(A.take(f"C{i}", (512,), F32), A.take(f"S{i}", (512,), F32)) for i in range(2)]
                RT = [dict(sq=A.take(f"sq{i}", (512,), BF16), qb=A.take(f"qb{i}", (512,), BF16), ln=A.take(f"ln{i}", (512,), F32),
                           t1=A.take(f"t1{i}", (512,), F32), t2=A.take(f"t2{i}", (512,), F32)) for i in range(2)]
                STAT = [mk_stats(4, f"stA{i}") for i in range(2)]
                P.dma("sp", Wkv.ap, wb_in[:, COL_BK:COL_BK + 1024].rearrange("(c p) n -> p c n", p=128), reads=[D_["wb_in"]], writes=[Wkv.b])

                cnt = dict(pj=0, rope=0, ost=0, vst=0, wr=0, ev=0)

                def front_ab(e):
                    rms_front(lambda j, e=e: xr[s * SEQ + e * 512 + j * 128: s * SEQ + e * 512 + (j + 1) * 128, :],
                              XB, HB, HT[e % 2], STAT[e % 2], (0, 1), "ab")
                    c_, s_ = CS[e % 2]
                    P.dma("sp", c_.ap, cosT[:, e * 512:(e + 1) * 512], writes=[c_.b])
                    P.dma("sp", s_.ap, sinT[:, e * 512:(e + 1) * 512], writes=[s_.b])

                def proj_ab(e):
                    ht = HT[e % 2]
                    c_, s_ = CS[e % 2]
                    blocks = []
                    blocks.append(("kgrp", None))
                    if e < 8:
                        own = 2 <= e <= 5
                        near = e in (1, 6)
                        groups = []
                        for g in range(3):
                            if own:
                                groups.append(("aq", g))
                            if own or near or g == 2:
                                groups.append(("ak", g))
                                groups.append(("av", g))
                        if own:
                            for i in range(3):
                                groups.append(("bq", i))
                        for grp in groups:
                            blocks.append(("sgrp", grp))
                    for kind, grp in blocks:
                        if kind == "kgrp":
                            wt, wcol0, rope, gi = Wkv, 0, ("norope" not in DBG), 2
                        else:
                            nm, g = grp
                            col0 = {"aq": COL_AQ, "ak": COL_AK, "av": COL_AV, "bq": COL_BQ}[nm] + g * 512
                            wt = WR[cnt["wr"] % 2]
                            cnt["wr"] += 1
                            P.dma("sp", wt.ap, wb_in[:, col0:col0 + 512].rearrange("(c p) n -> p c n", p=128), reads=[D_["wb_in"]], writes=[wt.b])
                            wcol0, rope, gi = 0, (nm == "bq" and "norope" not in DBG), 0
                        ost = OST[cnt["ost"] % 3]
                        cnt["ost"] += 1
                        pend = None
                        for hh in range(5):
                            if hh < 4:
                                bk = 2 + cnt["pj"] % 2
                                cnt["pj"] += 1
                                for c in range(16):
                                    P.op("pe", lambda e_, bk=bk, wt=wt, c=c, hh=hh, wcol0=wcol0, ht=ht: e_.matmul(
                                        PB[bk].ap, lhsT=wt.ap[:, c, wcol0 + hh * 128: wcol0 + (hh + 1) * 128], rhs=ht.ap[:, c, :], start=(c == 0), stop=(c == 15)),
                                        reads=[wt.b, ht.b], writes=[PB[bk].b])
                                if rope:
                                    rt = RT[cnt["rope"] % 2]
                                    sb_, qb_ = 4 + cnt["rope"] % 2, 6 + cnt["rope"] % 2
                                    cnt["rope"] += 1
                                    if ROPEN >= 1:
                                        P.op("act", lambda e_, bk=bk, rt=rt: e_.activation(out=rt["sq"].ap, in_=PB[bk].ap, func=AF.Square), reads=[PB[bk].b], writes=[rt["sq"].b])
                                    if ROPEN >= 2:
                                        P.op("dve", lambda e_, bk=bk, rt=rt: e_.tensor_copy(out=rt["qb"].ap, in_=PB[bk].ap), reads=[PB[bk].b], writes=[rt["qb"].b])
                                    cur = (bk, rt, sb_, qb_, hh)
                                else:
                                    dst = ost.ap[:, hh, :]
                                    if cnt["ev"] % 2 == 0 or "alldve" in DBG:
                                        P.op("dve", lambda e_, bk=bk, dst=dst: e_.tensor_copy(out=dst, in_=PB[bk].ap), reads=[PB[bk].b], writes=[ost.b])
                                    else:
                                        P.op("act", lambda e_, bk=bk, dst=dst: e_.activation(out=dst, in_=PB[bk].ap, func=AF.Copy), reads=[PB[bk].b], writes=[ost.b])
                                    cnt["ev"] += 1
                                    cur = None
                            else:
                                cur = None
                            if pend is not None:
                                bk0, rt, sb_, qb_, h0 = pend
                                if ROPEN >= 3:
                                    P.op("pe", lambda e_, sb_=sb_, rt=rt: e_.matmul(PB[sb_].ap, lhsT=ones.ap, rhs=rt["sq"].ap, start=True, stop=True),
                                         reads=[ones.b, rt["sq"].b], writes=[PB[sb_].b])
                                if ROPEN >= 4:
                                    P.op("pe", lambda e_, qb_=qb_, rt=rt: e_.matmul(PB[qb_].ap, lhsT=rm.ap, rhs=rt["qb"].ap, start=True, stop=True),
                                         reads=[rm.b, rt["qb"].b], writes=[PB[qb_].b])
                                if ROPEN >= 5:
                                    P.op("act", lambda e_, sb_=sb_, rt=rt: e_.activation(out=rt["ln"].ap, in_=PB[sb_].ap, func=AF.Ln, scale=1.0 / 128, bias=epsc.ap[:, 0:1]),
                                         reads=[PB[sb_].b, epsc.b], writes=[rt["ln"].b])
                                if ROPEN >= 6:
                                    P.op("act", lambda e_, rt=rt: e_.activation(out=rt["ln"].ap, in_=rt["ln"].ap, func=AF.Exp, scale=-0.5),
                                         reads=[rt["ln"].b], writes=[rt["ln"].b])
                                if ROPEN >= 7:
                                    P.op("dve", lambda e_, bk0=bk0, rt=rt, gi=gi: e_.scalar_tensor_tensor(out=rt["t1"].ap, in0=PB[bk0].ap, scalar=gcol.ap[:, gi:gi + 1], in1=c_.ap, op0=ALU.mult, op1=ALU.mult),
                                         reads=[PB[bk0].b, gcol.b, c_.b], writes=[rt["t1"].b])
                                if ROPEN >= 8:
                                    P.op("dve", lambda e_, qb_=qb_, rt=rt, gi=gi: e_.scalar_tensor_tensor(out=rt["t2"].ap, in0=PB[qb_].ap, scalar=gcol.ap[:, gi + 1:gi + 2], in1=s_.ap, op0=ALU.mult, op1=ALU.mult),
                                         reads=[PB[qb_].b, gcol.b, s_.b], writes=[rt["t2"].b])
                                if ROPEN >= 9:
                                    P.op("pool", lambda e_, rt=rt: e_.tensor_tensor(out=rt["t1"].ap, in0=rt["t1"].ap, in1=rt["t2"].ap, op=ALU.add),
                                         reads=[rt["t1"].b, rt["t2"].b], writes=[rt["t1"].b])
                                if ROPEN >= 10:
                                    P.op("dve", lambda e_, rt=rt, h0=h0, ost=ost: e_.tensor_tensor(out=ost.ap[:, h0, :], in0=rt["t1"].ap, in1=rt["ln"].ap, op=ALU.mult),
                                         reads=[rt["t1"].b, rt["ln"].b], writes=[ost.b])
                                else:
                                    P.op("dve", lambda e_, bk0=bk0, h0=h0, ost=ost: e_.tensor_copy(out=ost.ap[:, h0, :], in_=PB[bk0].ap), reads=[PB[bk0].b], writes=[ost.b])
                            pend = cur
                        if kind == "kgrp":
                          if "nostore" not in DBG:
                            P.dma("pool", KT[s * 4:(s + 1) * 4, :, e * 512:(e + 1) * 512].rearrange("h p n -> p h n"), ost.ap, reads=[ost.b], writes=[D_["KT"]])
                            vst = VST[cnt["vst"] % 2]
                            cnt["vst"] += 1
                            for j in range(4):
                                bk = 2 + cnt["pj"] % 2
                                cnt["pj"] += 1
                                for c in range(16):
                                    P.op("pe", lambda e_, bk=bk, c=c, j=j, ht=ht: e_.matmul(
                                        PB[bk].ap, lhsT=ht.ap[:, c, j * 128:(j + 1) * 128], rhs=Wkv.ap[:, c, 512:1024], start=(c == 0), stop=(c == 15)),
                                        reads=[Wkv.b, ht.b], writes=[PB[bk].b])
                                if j % 2 == 0:
                                    P.op("dve", lambda e_, bk=bk, j=j, vst=vst: e_.tensor_copy(out=vst.ap[:, j, :], in_=PB[bk].ap), reads=[PB[bk].b], writes=[vst.b])
                                else:
                                    P.op("act", lambda e_, bk=bk, j=j, vst=vst: e_.activation(out=vst.ap[:, j, :], in_=PB[bk].ap, func=AF.Copy), reads=[PB[bk].b], writes=[vst.b])
                            for kh in range(4 if "nostore" not in DBG else 0):
                                P.dma("pool", VG[s * 4 + kh, :, 4 * e:4 * e + 4, :], vst.ap[:, :, kh * 128:(kh + 1) * 128], reads=[vst.b], writes=[D_["VG"]])
                        else:
                            nm, g = grp
                            if nm == "aq":
                                dst = AQ[s * 12 + 4 * g: s * 12 + 4 * g + 4, :, (e - 2) * 512:(e - 1) * 512]
                                db = D_["AQ"]
                            elif nm == "ak":
                                dst = AK[s * 12 + 4 * g: s * 12 + 4 * g + 4, :, e * 512:(e + 1) * 512]
                                db = D_["AK"]
                            elif nm == "av":
                                dst = AV[s * 12 + 4 * g: s * 12 + 4 * g + 4, :, e * 512:(e + 1) * 512]
                                db = D_["AV"]
                            else:
                                dst = BQ[s * 12 + 4 * g: s * 12 + 4 * g + 4, :, (e - 2) * 512:(e - 1) * 512]
                                db = D_["BQ"]
                            if "nostore" not in DBG:
                                P.dma("pool", dst.rearrange("h p n -> p h n"), ost.ap, reads=[ost.b], writes=[db])

                NTL = ntl
                front_ab(0)
                for e in range(NTL):
                    if e + 1 < NTL:
                        front_ab(e + 1)
                    proj_ab(e)
                P.barrier()

            if "C" in phases:
                A.reset(pm)
                ABT = A.take("ABT", (12, 2, 128), F32)
                P.dma("sp", ABT.ap, abt.rearrange("p (h a q) -> p h a q", h=12, a=2), writes=[ABT.b])
                QTOK = [A.take(f"qtok{i}", (2048,), BF16) for i in range(2)]
                KTOK = [A.take(f"ktok{i}", (4096,), BF16) for i in range(2)]
                VTOK = [A.take(f"vtok{i}", (4096,), BF16) for i in range(2)]
                QR = A.take("qr", (2048,), BF16)
                KR = A.take("kr", (4096,), BF16)
                VR = A.take("vr", (4096,), BF16)
                VT = A.take("vt", (32, 128), BF16)
                SBT = [A.take(f"sbt{i}", (4, 2, 128), F32) for i in range(2)]
                PT = [A.take(f"ptc{i}", (4, 2, 128), BF16) for i in range(2)]
                NUM = A.take("num", (2048,), F32)
                DEN = A.take("den", (2048,), F32)
                YA = [A.take(f"ya{i}", (2048,), BF16) for i in range(2)]
                ucnt = 0
                bcnt = 0
                for hs in range(4):
                    for g in range(3):
                        d = DIL[g]
                        Lr = CH // d
                        H = 64 * d
                        Wd = Lr + 128
                        hd = 4 * g + hs
                        qt_, kt_, vt_ = QTOK[ucnt % 2], KTOK[ucnt % 2], VTOK[ucnt % 2]
                        ucnt += 1
                        P.dma("sp", qt_.ap, AQ[s * 12 + hd, :, :], reads=[D_["AQ"]], writes=[qt_.b])
                        P.dma("sp", kt_.ap[:, :d * Wd], AK[s * 12 + hd, :, 1024 - H:3072 + H], reads=[D_["AK"]], writes=[kt_.b])
                        P.dma("sp", vt_.ap[:, :d * Wd], AV[s * 12 + hd, :, 1024 - H:3072 + H], reads=[D_["AV"]], writes=[vt_.b])
                        qr3 = QR.ap.rearrange("p (r j) -> p r j", r=d)
                        kr3 = KR.ap[:, :d * Wd].rearrange("p (r j) -> p r j", r=d)
                        vr3 = VR.ap[:, :d * Wd].rearrange("p (r j) -> p r j", r=d)
                        P.op("dve", lambda e, qt_=qt_, qr3=qr3, d=d: e.tensor_copy(out=qr3, in_=qt_.ap.rearrange("p (j r) -> p r j", r=d)), reads=[qt_.b], writes=[QR.b])
                        P.op("pool", lambda e, kt_=kt_, kr3=kr3, d=d, Wd=Wd: e.tensor_copy(out=kr3, in_=kt_.ap[:, :d * Wd].rearrange("p (j r) -> p r j", r=d)), reads=[kt_.b], writes=[KR.b])
                        P.op("act", lambda e, vt_=vt_, vr3=vr3, d=d, Wd=Wd: e.activation(out=vr3, in_=vt_.ap[:, :d * Wd].rearrange("p (j r) -> p r j", r=d), func=AF.Copy), reads=[vt_.b], writes=[VR.b])
                        nper = Lr // 128 + 1
                        nkt = d * nper
                        for b0 in range(0, nkt, 8):
                            nb = min(8, nkt - b0)
                            bk = 0 if (b0 // 8) % 2 == 0 else 2
                            for i in range(nb):
                                kti = b0 + i
                                r, ii = kti // nper, kti % nper
                                P.op("pe", lambda e, bk=bk, i=i, r=r, ii=ii, vr3=vr3: e.transpose(out=pb_bf(bk)[:, i, :], in_=vr3[:, r, ii * 128:(ii + 1) * 128], identity=ident.ap),
                                     reads=[VR.b, ident.b], writes=[PB[bk].b])
                            P.op("dve", lambda e, bk=bk, b0=b0, nb=nb: e.tensor_copy(out=VT.ap[:, b0:b0 + nb, :], in_=pb_bf(bk)[:, :nb, :]), reads=[PB[bk].b], writes=[VT.b])
                        for b in range(4):
                            sl = bcnt % 2
                            bcnt += 1
                            sb0, sb1 = (0, 1) if sl == 0 else (2, 3)
                            ob, db = (4, 6) if sl == 0 else (5, 7)
                            sbt, pt = SBT[sl], PT[sl]
                            tiles = []
                            for tt in range(4):
                                t = 4 * b + tt
                                r = (128 * t) // Lr
                                qi = ((128 * t) % Lr) // 128
                                tiles.append((r, qi))
                                bank = sb0 if tt < 2 else sb1
                                for ab in range(2):
                                    col = ((tt % 2) * 2 + ab) * 128
                                    P.op("pe", lambda e, bank=bank, col=col, r=r, qi=qi, ab=ab, kr3=kr3, qr3=qr3: e.matmul(
                                        PB[bank].ap[:, col:col + 128], lhsT=kr3[:, r, (qi + ab) * 128:(qi + ab + 1) * 128], rhs=qr3[:, r, qi * 128:(qi + 1) * 128], start=True, stop=True),
                                        reads=[KR.b, QR.b], writes=[PB[bank].b])
                            for half in range(2):
                                bank = sb0 if half == 0 else sb1
                                P.op("dve", lambda e, bank=bank, half=half, sbt=sbt, hd=hd: e.scalar_tensor_tensor(
                                    out=sbt.ap[:, half * 2:half * 2 + 2, :, :], in0=PB[bank].ap.rearrange("p (t a q) -> p t a q", t=2, a=2), scalar=SCALE,
                                    in1=ABT.ap[:, hd, :, :].unsqueeze(1).to_broadcast([128, 2, 2, 128]), op0=ALU.mult, op1=ALU.add),
                                    reads=[PB[bank].b, ABT.b], writes=[sbt.b])
                            for tt in range(4):
                                r, qi = tiles[tt]
                                for ab in range(2):
                                    kti = KT_BASE[g] + r * nper + qi + ab
                                    P.op("act", lambda e, tt=tt, ab=ab, kti=kti, sbt=sbt, pt=pt: e.activation(out=pt.ap[:, tt, ab, :], in_=sbt.ap[:, tt, ab, :], func=AF.Exp, bias=kv.ap[:, kti:kti + 1]),
                                         reads=[sbt.b, kv.b], writes=[pt.b])
                            for tt in range(4):
                                r, qi = tiles[tt]
                                for ab in range(2):
                                    kl = r * nper + qi + ab
                                    P.op("pe", lambda e, ob=ob, tt=tt, ab=ab, kl=kl, pt=pt: e.matmul(PB[ob].ap[:, tt * 128:(tt + 1) * 128], lhsT=VT.ap[:, kl, :], rhs=pt.ap[:, tt, ab, :], start=(ab == 0), stop=(ab == 1)),
                                         reads=[VT.b, pt.b], writes=[PB[ob].b])
                                for ab in range(2):
                                    P.op("pe", lambda e, db=db, tt=tt, ab=ab, pt=pt: e.matmul(PB[db].ap[:, tt * 128:(tt + 1) * 128], lhsT=ones.ap, rhs=pt.ap[:, tt, ab, :], start=(ab == 0), stop=(ab == 1)),
                                         reads=[ones.b, pt.b], writes=[PB[db].b])
                            if d == 1:
                                r0, nr, j0, nj = 0, 1, 512 * b, 512
                            elif d == 4:
                                r0, nr, j0, nj = b, 1, 0, 512
                            else:
                                r0, nr, j0, nj = 4 * b, 4, 0, 128
                            nview = NUM.ap.rearrange("p (j r) -> p r j", r=d)[:, r0:r0 + nr, j0:j0 + nj]
                            dview = DEN.ap.rearrange("p (j r) -> p r j", r=d)[:, r0:r0 + nr, j0:j0 + nj]
                            osrc = PB[ob].ap.rearrange("p (r j) -> p r j", r=nr)
                            dsrc = PB[db].ap.rearrange("p (r j) -> p r j", r=nr)
                            if g == 0:
                                P.op("dve", lambda e, nview=nview, osrc=osrc: e.tensor_copy(out=nview, in_=osrc), reads=[PB[ob].b], writes=[NUM.b])
                                P.op("act", lambda e, dview=dview, dsrc=dsrc: e.activation(out=dview, in_=dsrc, func=AF.Copy), reads=[PB[db].b], writes=[DEN.b])
                            else:
                                P.op("dve", lambda e, nview=nview, osrc=osrc: e.tensor_tensor(out=nview, in0=osrc, in1=nview, op=ALU.add), reads=[PB[ob].b, NUM.b], writes=[NUM.b])
                                P.op("dve", lambda e, dview=dview, dsrc=dsrc: e.tensor_tensor(out=dview, in0=dsrc, in1=dview, op=ALU.add), reads=[PB[db].b, DEN.b], writes=[DEN.b])
                    ya = YA[hs % 2]
                    P.op("dve", lambda e: e.reciprocal(out=DEN.ap, in_=DEN.ap), reads=[DEN.b], writes=[DEN.b])
                    P.op("dve", lambda e, ya=ya: e.tensor_tensor(out=ya.ap, in0=NUM.ap, in1=DEN.ap, op=ALU.mult), reads=[NUM.b, DEN.b], writes=[ya.b])
                    P.dma("pool", YT[s * 16 + hs, :, :], ya.ap, reads=[ya.b], writes=[D_["YT"]])
                P.barrier()

            if "D" in phases:
                A.reset(pm)
                KH = [A.take(f"kh{i}", (8192,), BF16) for i in range(2)]
                VH = [A.take(f"vh{i}", (64, 128), BF16) for i in range(2)]
                QB_ = [A.take(f"qd{i}", (512,), BF16) for i in range(3)]
                PTD = [A.take(f"ptd{i}", (1024,), BF16) for i in range(2)]
                RD = [A.take(f"rd{i}", (512,), F32) for i in range(2)]
                YS = [A.take(f"ys{i}", (512,), BF16) for i in range(2)]
                items = []
                blocks = [(kh, j, qt) for kh in range(4) for j in range(3) for qt in range(4)]
                for bi, (kh, j, qt) in enumerate(blocks):
                    for kp in range(64):
                        items.append((bi, kh, j, qt, kp))
                NI = len(items)

                def stage_qk(it):
                    bi, kh, j, qt, kp = items[it]
                    if kp == 0:
                        if j == 0 and qt == 0:
                            for hf in range(2):
                                P.dma("sp", KH[hf].ap, KT[s * 4 + kh, :, hf * 8192:(hf + 1) * 8192], reads=[D_["KT"]], writes=[KH[hf].b])
                                P.dma("sp", VH[hf].ap, VG[s * 4 + kh, :, hf * 64:(hf + 1) * 64, :], reads=[D_["VG"]], writes=[VH[hf].b])
                        qb = QB_[bi % 3]
                        P.dma("sp", qb.ap, BQ[s * 12 + kh * 3 + j, :, qt * 512:(qt + 1) * 512], reads=[D_["BQ"]], writes=[qb.b])
                    qb = QB_[bi % 3]
                    sl = it % 2
                    for hh in range(2):
                        ktile = 2 * kp + hh
                        hf, ko = ktile // 64, (ktile % 64) * 128
                        bank = 2 * sl + hh
                        P.op("pe", lambda e, bank=bank, hf=hf, ko=ko, qb=qb: e.matmul(PB[bank].ap, lhsT=KH[hf].ap[:, ko:ko + 128], rhs=qb.ap, start=True, stop=True),
                             reads=[KH[hf].b, qb.b], writes=[PB[bank].b])
                    pt = PTD[sl]
                    P.op("act", lambda e, sl=sl, pt=pt: e.activation(out=pt.ap, in_=psum_t[:, 1024 * sl:1024 * (sl + 1)], func=AF.Exp, scale=SCALE),
                         reads=[PB[2 * sl].b, PB[2 * sl + 1].b], writes=[pt.b])

                def stage_pv(it):
                    bi, kh, j, qt, kp = items[it]
                    sl = it % 2
                    pt = PTD[sl]
                    ob, db = 4 + bi % 2, 6 + bi % 2
                    for hh in range(2):
                        ktile = 2 * kp + hh
                        hf, kl = ktile // 64, ktile % 64
                        first = (kp == 0 and hh == 0)
                        last = (kp == 63 and hh == 1)
                        P.op("pe", lambda e, ob=ob, hf=hf, kl=kl, hh=hh, pt=pt, first=first, last=last: e.matmul(PB[ob].ap, lhsT=VH[hf].ap[:, kl, :], rhs=pt.ap[:, hh * 512:(hh + 1) * 512], start=first, stop=last),
                             reads=[VH[hf].b, pt.b], writes=[PB[ob].b])
                        P.op("pe", lambda e, db=db, hh=hh, pt=pt, first=first, last=last: e.matmul(PB[db].ap, lhsT=ones.ap, rhs=pt.ap[:, hh * 512:(hh + 1) * 512], start=first, stop=last),
                             reads=[ones.b, pt.b], writes=[PB[db].b])
                    if kp == 63:
                        rd, ys = RD[bi % 2], YS[bi % 2]
                        P.op("dve", lambda e, db=db, rd=rd: e.reciprocal(out=rd.ap, in_=PB[db].ap), reads=[PB[db].b], writes=[rd.b])
                        P.op("dve", lambda e, ob=ob, rd=rd, ys=ys: e.tensor_tensor(out=ys.ap, in0=PB[ob].ap, in1=rd.ap, op=ALU.mult), reads=[PB[ob].b, rd.b], writes=[ys.b])
                        P.dma("pool", YT[s * 16 + 4 + kh * 3 + j, :, qt * 512:(qt + 1) * 512], ys.ap, reads=[ys.b], writes=[D_["YT"]])

                for it in range(NI + 1):
                    if it < NI:
                        stage_qk(it)
                    if it >= 1:
                        stage_pv(it - 1)
                P.barrier()

            if "E" in phases:
                A.reset(pm)
                GF = A.take("gf", (2048,), F32)
                P.dma("sp", GF.ap, g_fin[0:1, :].to_broadcast([128, DM]), writes=[GF.b])
                X1 = [A.take(f"x1_{i}", (2048,), F32) for i in range(4)]
                HBE = [A.take(f"hbe{i}", (2048,), BF16) for i in range(2)]
                HTE = A.take("hte", (16, 512), BF16)
                YTL = A.take("ytl", (16, 512), BF16)
                MG = A.take("mg", (16, 512), BF16)
                UT = A.take("ut", (32, 512), BF16)
                WE = [A.take(f"we{i}", (16, 512), BF16) for i in range(3)]
                SG = [(A.take(f"sga{i}", (512,), F32), A.take(f"sgb{i}", (512,), F32), A.take(f"tm{i}", (512,), F32)) for i in range(2)]
                RL = [A.take(f"rl{i}", (512,), F32) for i in range(2)]
                OUTS = [A.take(f"outs{i}", (2048,), F32) for i in range(1)]
                STE = mk_stats(4, "stE")
                wcnt = [0]

                def wload(src_ap, dbuf):
                    wt = WE[wcnt[0] % 3]
                    wcnt[0] += 1
                    P.dma("sp", wt.ap, src_ap, reads=[dbuf], writes=[wt.b])
                    return wt

                for ti in range(4):
                    e = 2 + ti
                    tok0 = ti * 512
                    rms_front(lambda j, e=e: xr[s * SEQ + e * 512 + j * 128: s * SEQ + e * 512 + (j + 1) * 128, :],
                              X1, HBE, HTE, STE, (0, 1), "e1")
                    P.dma("sp", YTL.ap, YT[s * 16:(s + 1) * 16, :, tok0:tok0 + 512].rearrange("c p n -> p c n"), reads=[D_["YT"]], writes=[YTL.b])
                    gcnt = 0
                    for fg in range(4):
                        wga = wload(wb_in[:, COL_GA + fg * 512: COL_GA + (fg + 1) * 512].rearrange("(c p) n -> p c n", p=128), D_["wb_in"])
                        wgb = wload(wb_in[:, COL_GB + fg * 512: COL_GB + (fg + 1) * 512].rearrange("(c p) n -> p c n", p=128), D_["wb_in"])
                        wbr = wload(wb_br[:, fg * 512:(fg + 1) * 512].rearrange("(c p) n -> p c n", p=128), D_["wb_br"])
                        for fb in range(4):
                            f = fg * 4 + fb
                            base = 4 * (gcnt % 2)
                            sga, sgb, tm = SG[gcnt % 2]
                            gcnt += 1
                            bga, bgb, boa, bob = base, base + 1, base + 2, base + 3
                            for c in range(16):
                                P.op("pe", lambda e_, c=c, fb=fb, wga=wga, bga=bga: e_.matmul(PB[bga].ap, lhsT=wga.ap[:, c, fb * 128:(fb + 1) * 128], rhs=HTE.ap[:, c, :], start=(c == 0), stop=(c == 15)),
                                     reads=[wga.b, HTE.b], writes=[PB[bga].b])
                            for c in range(16):
                                P.op("pe", lambda e_, c=c, fb=fb, wgb=wgb, bgb=bgb: e_.matmul(PB[bgb].ap, lhsT=wgb.ap[:, c, fb * 128:(fb + 1) * 128], rhs=HTE.ap[:, c, :], start=(c == 0), stop=(c == 15)),
                                     reads=[wgb.b, HTE.b], writes=[PB[bgb].b])
                            for c in range(4):
                                P.op("pe", lambda e_, c=c, fb=fb, wbr=wbr, boa=boa: e_.matmul(PB[boa].ap, lhsT=wbr.ap[:, c, fb * 128:(fb + 1) * 128], rhs=YTL.ap[:, c, :], start=(c == 0), stop=(c == 3)),
                                     reads=[wbr.b, YTL.b], writes=[PB[boa].b])
                            for c in range(4, 16):
                                P.op("pe", lambda e_, c=c, fb=fb, wbr=wbr, bob=bob: e_.matmul(PB[bob].ap, lhsT=wbr.ap[:, c, fb * 128:(fb + 1) * 128], rhs=YTL.ap[:, c, :], start=(c == 4), stop=(c == 15)),
                                     reads=[wbr.b, YTL.b], writes=[PB[bob].b])
                            P.op("act", lambda e_, bga=bga, sga=sga: e_.activation(out=sga.ap, in_=PB[bga].ap, func=AF.Sigmoid), reads=[PB[bga].b], writes=[sga.b])
                            P.op("act", lambda e_, bgb=bgb, sgb=sgb: e_.activation(out=sgb.ap, in_=PB[bgb].ap, func=AF.Sigmoid), reads=[PB[bgb].b], writes=[sgb.b])
                            P.op("dve", lambda e_, boa=boa, sga=sga, tm=tm: e_.tensor_tensor(out=tm.ap, in0=PB[boa].ap, in1=sga.ap, op=ALU.mult), reads=[PB[boa].b, sga.b], writes=[tm.b])
                            P.op("dve", lambda e_, bob=bob, sgb=sgb: e_.tensor_tensor(out=sgb.ap, in0=PB[bob].ap, in1=sgb.ap, op=ALU.mult), reads=[PB[bob].b, sgb.b], writes=[sgb.b])
                            P.op("pool", lambda e_, f=f, tm=tm, sgb=sgb: e_.tensor_tensor(out=MG.ap[:, f, :], in0=tm.ap, in1=sgb.ap, op=ALU.add), reads=[tm.b, sgb.b], writes=[MG.b])
                    pcnt = 0
                    for cb in range(4):
                        wo = wload(wb_out[:, cb * 512:(cb + 1) * 512].rearrange("(c p) n -> p c n", p=128), D_["wb_out"])
                        for sub in range(4):
                            bk = pcnt % 4
                            pcnt += 1
                            for c in range(16):
                                P.op("pe", lambda e_, c=c, sub=sub, wo=wo, bk=bk: e_.matmul(PB[bk].ap, lhsT=MG.ap[:, c, sub * 128:(sub + 1) * 128], rhs=wo.ap[:, c, :], start=(c == 0), stop=(c == 15)),
                                     reads=[MG.b, wo.b], writes=[PB[bk].b])
                            xs = X1[sub]
                            P.op("dve", lambda e_, bk=bk, xs=xs, cb=cb: e_.tensor_tensor(out=xs.ap[:, cb * 512:(cb + 1) * 512], in0=PB[bk].ap, in1=xs.ap[:, cb * 512:(cb + 1) * 512], op=ALU.add),
                                 reads=[PB[bk].b, xs.b], writes=[xs.b])
                    rms_front(None, X1, HBE, HTE, STE, (4, 5), "e4")
                    for hf in range(2):
                        ecnt = 0
                        for fg in range(8):
                            col0 = hf * 4096 + fg * 512
                            w1 = wload(wb_ff1[:, col0:col0 + 512].rearrange("(c p) n -> p c n", p=128), D_["wb_ff1"])
                            for fb in range(4):
                                bk = ecnt % 4
                                rl = RL[ecnt % 2]
                                ecnt += 1
                                for c in range(16):
                                    P.op("pe", lambda e_, c=c, fb=fb, w1=w1, bk=bk: e_.matmul(PB[bk].ap, lhsT=w1.ap[:, c, fb * 128:(fb + 1) * 128], rhs=HTE.ap[:, c, :], start=(c == 0), stop=(c == 15)),
                                         reads=[w1.b, HTE.b], writes=[PB[bk].b])
                                P.op("act", lambda e_, bk=bk, rl=rl: e_.activation(out=rl.ap, in_=PB[bk].ap, func=AF.Relu), reads=[PB[bk].b], writes=[rl.b])
                                ub = fg * 4 + fb
                                if ecnt % 2 == 0:
                                    P.op("dve", lambda e_, rl=rl, ub=ub: e_.tensor_tensor(out=UT.ap[:, ub, :], in0=rl.ap, in1=rl.ap, op=ALU.mult), reads=[rl.b], writes=[UT.b])
                                else:
                                    P.op("pool", lambda e_, rl=rl, ub=ub: e_.tensor_tensor(out=UT.ap[:, ub, :], in0=rl.ap, in1=rl.ap, op=ALU.mult), reads=[rl.b], writes=[UT.b])
                        for cb in range(4):
                            w2 = []
                            for kq in range(2):
                                r0 = hf * 4096 + kq * 2048
                                w2.append(wload(wb_ff2[r0:r0 + 2048, cb * 512:(cb + 1) * 512].rearrange("(c p) n -> p c n", p=128), D_["wb_ff2"]))
                            for sub in range(4):
                                bk = 4 + sub
                                for kq in range(2):
                                    for c in range(16):
                                        uc = kq * 16 + c
                                        P.op("pe", lambda e_, c=c, uc=uc, sub=sub, kq=kq, w2=w2, bk=bk: e_.matmul(PB[bk].ap, lhsT=UT.ap[:, uc, sub * 128:(sub + 1) * 128], rhs=w2[kq].ap[:, c, :], start=(uc == 0), stop=(uc == 31)),
                                             reads=[UT.b, w2[kq].b], writes=[PB[bk].b])
                                xs = X1[sub]
                                P.op("dve", lambda e_, bk=bk, xs=xs, cb=cb: e_.tensor_tensor(out=xs.ap[:, cb * 512:(cb + 1) * 512], in0=PB[bk].ap, in1=xs.ap[:, cb * 512:(cb + 1) * 512], op=ALU.add),
                                     reads=[PB[bk].b, xs.b], writes=[xs.b])
                    for sub in range(4):
                        xs = X1[sub]
                        ss, sd, rs = STE[sub]
                        outs = OUTS[0]
                        P.op("act", lambda e_, xs=xs, outs=outs, ss=ss: e_.activation(out=outs.ap, in_=xs.ap, func=AF.Square, accum_out=ss.ap), reads=[xs.b], writes=[outs.b, ss.b])
                        P.op("act", lambda e_, ss=ss, sd=sd: e_.activation(out=sd.ap, in_=ss.ap, func=AF.Sqrt, scale=1.0 / DM, bias=epsc.ap[:, 0:1]), reads=[ss.b, epsc.b], writes=[sd.b])
                        P.op("dve", lambda e_, sd=sd, rs=rs: e_.reciprocal(out=rs.ap, in_=sd.ap), reads=[sd.b], writes=[rs.b])
                        P.op("dve", lambda e_, xs=xs, outs=outs, rs=rs: e_.scalar_tensor_tensor(out=outs.ap, in0=xs.ap, scalar=rs.ap[:, 0:1], in1=GF.ap, op0=ALU.mult, op1=ALU.mult),
                             reads=[xs.b, rs.b, GF.b], writes=[outs.b])
                        r0 = s * CH + tok0 + sub * 128
                        P.dma("pool", y[r0:r0 + 128, :], outs.ap, reads=[outs.b], writes=[D_["y"]])
                P.barrier()

        P.emit_all([])
    return nc


def _const_tables():
    n = 12
    slopes = 2.0 ** (-8.0 * np.arange(1, n + 1, dtype=np.float64) / n)
    i = np.arange(128)[:, None]
    j = np.arange(128)[None, :]
    ab = np.zeros((128, 12, 2, 128), np.float32)
    for g in range(3):
        for hs in range(4):
            hd = 4 * g + hs
            for a, rel in enumerate((i - 64 - j, i + 64 - j)):
                v = -slopes[hd] * DIL[g] * np.abs(rel)
                ab[:, hd, a, :] = np.where(np.abs(rel) <= 64, v, NEG)
    rmat = np.zeros((128, 128), np.float32)
    for m in range(128):
        pr = m + 32 if (m % 64) < 32 else m - 32
        rmat[pr, m] = 1.0
    return ab.reshape(128, 12 * 256), rmat


def _rope_tables(c):
    u = np.arange(SEQ)
    t = (c * CH - 1024 + u) % SEQ
    row = (t // 64).astype(np.float32)
    col = (t % 64).astype(np.float32)
    half = 64
    inv = (10000.0 ** (-np.arange(0, half, 2, dtype=np.float32) / half)).astype(np.float32)
    ang_r = row[None, :] * inv[:, None]
    ang_c = col[None, :] * inv[:, None]
    cosT = np.concatenate([np.cos(ang_r), np.cos(ang_r), np.cos(ang_c), np.cos(ang_c)], axis=0).astype(np.float32)
    sinT = np.concatenate([-np.sin(ang_r), np.sin(ang_r), -np.sin(ang_c), np.sin(ang_c)], axis=0).astype(np.float32)
    return np.ascontiguousarray(cosT), np.ascontiguousarray(sinT)


def _kvalid(c):
    kvd = np.zeros((128, 72), np.float32)
    m = np.arange(128)
    for g in range(3):
        d = DIL[g]
        Lr = CH // d
        nper = Lr // 128 + 1
        for r in range(d):
            for i in range(nper):
                t = c * CH - 64 * d + (128 * i + m) * d + r
                kvd[:, KT_BASE[g] + r * nper + i] = np.where((t >= 0) & (t < SEQ), 0.0, NEG)
    return kvd


_NC_CACHE = {}


def kernel(x_prompt, x_sample, g_mix, w_in, g_q, g_k, w_branch, w_out, g_mlp, w_ff1, w_ff2, g_final):
    f = lambda a: np.ascontiguousarray(np.asarray(a, dtype=np.float32))
    seqs = [f(x_prompt)[0], f(x_sample)[0], f(x_sample)[1]]
    if "nc" not in _NC_CACHE:
        _NC_CACHE["nc"] = build()
    nc = _NC_CACHE["nc"]
    ab, rmat = _const_tables()
    common = {
        "w_in": f(w_in)[0], "w_branch": f(w_branch)[0], "w_out": f(w_out)[0], "w_ff1": f(w_ff1)[0], "w_ff2": f(w_ff2)[0],
        "g_mix": f(g_mix).reshape(1, DM), "g_mlp": f(g_mlp).reshape(1, DM), "g_final": f(g_final).reshape(1, DM),
        "g_q": f(g_q).reshape(1, 128), "g_k": f(g_k).reshape(1, 128), "abt": ab, "rmat": rmat,
    }
    in_maps = []
    for c in range(NCORE):
        sh = c * CH - 1024
        xr = np.concatenate([np.roll(sq, -sh, axis=0) for sq in seqs], axis=0)
        cosT, sinT = _rope_tables(c)
        m = dict(common)
        m.update({"xr": xr, "cosT": cosT, "sinT": sinT, "kval": _kvalid(c)})
        in_maps.append(m)
    res = run_bass_kernel_spmd(nc, in_maps, core_ids=list(range(NCORE)))
    outs = [np.empty((SEQ, DM), np.float32) for _ in range(3)]
    for c in range(NCORE):
        yc = res.results[c]["y"]
        for s in range(3):
            outs[s][c * CH:(c + 1) * CH] = yc[s * CH:(s + 1) * CH]
    y_prompt = outs[0][None]
    y_sample = np.stack([outs[1], outs[2]], axis=0)
    return (y_prompt, y_sample)
```

```python
import contextlib
import math
import numpy as np
import concourse.bass as bass
import concourse.mybir as mybir
from concourse.bass_utils import run_bass_kernel_spmd

F32 = mybir.dt.float32
BF16 = mybir.dt.bfloat16
AF = mybir.ActivationFunctionType
ALU = mybir.AluOpType

SEQ = 16384
DM = 2048
NCORE = 8
CH = 2048
NSEQ = 3
DIL = (1, 4, 16)
COL_AQ, COL_AK, COL_AV, COL_BQ, COL_BK, COL_GA, COL_GB = 0, 1536, 3072, 4608, 6144, 7168, 9216
IN_W = 11264
DFF = 8192
EPS = 1e-6
NEG = -30000.0
SCALE = 128 ** -0.5
KT_BASE = (0, 17, 37)
NKT = 69

COMPUTE = ("pe", "act", "dve", "pool")
EPOCH = 12000


class Buf:
    __slots__ = ("name", "w", "r", "excl")

    def __init__(self, name="", excl=False):
        self.name = name
        self.w = None
        self.r = []
        self.excl = excl


class Op:
    __slots__ = ("eng", "emit", "deps", "marked", "dma", "sem", "val", "slot_prev")

    def __init__(self, eng, emit, dma):
        self.eng = eng
        self.emit = emit
        self.deps = []
        self.marked = False
        self.dma = dma
        self.sem = None
        self.val = 0
        self.slot_prev = None


class Prog:
    def __init__(self, nc, dma_ring=16):
        self.nc = nc
        self.ops = {e: [] for e in ("pe", "act", "dve", "pool", "sp")}
        self.dma_ring = dma_ring
        self.dmas = []

    def op(self, eng, emit, reads=(), writes=(), dma=False):
        o = Op(eng, emit, dma)
        deps = {}
        writes = list(writes) + [b for b in reads if b.excl]
        reads = [b for b in reads if not b.excl]
        for b in reads:
            if b.w is not None:
                deps[id(b.w)] = (b.w, "raw")
        for b in writes:
            if b.w is not None and id(b.w) not in deps:
                deps[id(b.w)] = (b.w, "waw")
            for r in b.r:
                if id(r) not in deps:
                    deps[id(r)] = (r, "war")
        for d, kind in deps.values():
            if d is o:
                continue
            if (not dma) and (not d.dma) and d.eng == eng:
                if eng == "pe" or kind != "raw":
                    continue
            o.deps.append(d)
            d.marked = True
        for b in reads:
            b.r.append(o)
        for b in writes:
            b.w = o
            b.r = []
        self.ops[eng].append(o)
        if dma:
            self.dmas.append(o)
        return o

    def dma(self, q, out, in_, reads=(), writes=(), **kw):
        return self.op(q, lambda e: e.dma_start(out=out, in_=in_, **kw), reads, writes, dma=True)

    def barrier(self):
        lasts = []
        for e in COMPUTE:
            for o in reversed(self.ops[e]):
                if not o.dma and o.emit is not None:
                    lasts.append(o)
                    break
        pend = lasts + self.dmas
        self.dmas = []
        for e in ("act", "dve", "pool", "sp"):
            o = Op(e, None, False)
            for d in pend:
                if d.eng == e and not d.dma:
                    continue
                o.deps.append(d)
                d.marked = True
            self.ops[e].append(o)

    def emit_all(self, final_ops):
        nc = self.nc
        with contextlib.ExitStack() as st:
            eng_sems = {}
            for e in COMPUTE:
                cnt = 0
                ep = 0
                sems = [st.enter_context(nc.semaphore(f"s_{e}_0"))]
                for o in self.ops[e]:
                    if o.dma or not o.marked or o.emit is None:
                        continue
                    if cnt >= EPOCH:
                        ep += 1
                        cnt = 0
                        sems.append(st.enter_context(nc.semaphore(f"s_{e}_{ep}")))
                    cnt += 1
                    o.sem = (e, ep)
                    o.val = cnt
                eng_sems[e] = sems
            dma_sems = {}
            for q in ("sp", "act", "pool"):
                dl = [o for o in self.ops[q] if o.dma]
                if not dl:
                    continue
                nslot = min(self.dma_ring, len(dl))
                sems = [st.enter_context(nc.semaphore(f"d_{q}_{i}")) for i in range(nslot)]
                last = [None] * nslot
                vals = [0] * nslot
                for i, o in enumerate(dl):
                    s = i % nslot
                    vals[s] += 16
                    o.sem = (q + "_dma", s)
                    o.val = vals[s]
                    o.slot_prev = last[s]
                    last[s] = o
                dma_sems[q] = sems

            def sem_of(o):
                k, i = o.sem
                if k.endswith("_dma"):
                    return dma_sems[k[:-4]][i]
                return eng_sems[k][i]

            block = st.enter_context(nc.Block())
            handles = {"pe": block.tensor, "act": block.scalar, "dve": block.vector,
                       "pool": block.gpsimd, "sp": block.sync}

            def make(ename):
                ops = self.ops[ename]
                finals = final_ops if ename == "sp" else []

                def body(eng):
                    waited = {}

                    def wait_for(d):
                        k, i = d.sem
                        if k.endswith("_dma"):
                            key = (k, i)
                            if waited.get(key, 0) >= d.val:
                                return
                            waited[key] = d.val
                        else:
                            cur = waited.get(k, (-1, 0))
                            if cur >= (i, d.val):
                                return
                            waited[k] = (i, d.val)
                        eng.wait_ge(sem_of(d), d.val)

                    for o in ops:
                        if o.dma and o.slot_prev is not None:
                            wait_for(o.slot_prev)
                        for d in o.deps:
                            wait_for(d)
                        if o.emit is None:
                            continue
                        ins = o.emit(eng)
                        if o.dma:
                            ins.then_inc(sem_of(o), 16)
                        elif o.marked:
                            ins.then_inc(sem_of(o), 1)
                    for d in finals:
                        wait_for(d)
                return body

            for ename in ("sp", "act", "dve", "pool", "pe"):
                handles[ename](make(ename))


class T:
    __slots__ = ("ap", "b")

    def __init__(self, ap, name=""):
        self.ap = ap
        self.b = Buf(name)


class Arena:
    def __init__(self, base_ap, nwords):
        self.base = base_ap
        self.n = nwords
        self.top = 0

    def mark(self):
        return self.top

    def reset(self, m):
        self.top = m

    def take(self, name, free_shape, dtype):
        esz = 4 if dtype == F32 else 2
        nel = int(np.prod(free_shape))
        nw = (nel * esz + 3) // 4
        nw = (nw + 7) // 8 * 8
        off = self.top
        self.top += nw
        assert self.top <= self.n, f"arena overflow at {name}: {self.top*4} > {self.n*4}"
        v = self.base[:, off:off + (nel * esz) // 4]
        if dtype != F32:
            v = v.bitcast(dtype)
        if len(free_shape) > 1:
            names = [f"a{i}" for i in range(len(free_shape))]
            pat = "p (" + " ".join(names) + ") -> p " + " ".join(names)
            kw = {n: int(s) for n, s in zip(names[1:], free_shape[1:])}
            v = v.rearrange(pat, **kw)
        return T(v, name)


def build(nseq=NSEQ, debug=False, phases="ABCDE", ntl=SEQ // 512):
    nc = bass.Bass("TRN2", target_bir_lowering=False)
    import os
    DBG = os.environ.get("K_DBG", "").split(",")
    ROPEN = int(os.environ.get("K_ROPEN", "99"))

    def din(name, shape):
        return nc.dram_tensor(name, shape, F32, kind="ExternalInput").ap()

    xr = din("xr", [nseq * SEQ, DM])
    w_in = din("w_in", [DM, IN_W])
    w_br = din("w_branch", [DM, DM])
    w_out = din("w_out", [DM, DM])
    w_ff1 = din("w_ff1", [DM, DFF])
    w_ff2 = din("w_ff2", [DFF, DM])
    g_mix = din("g_mix", [1, DM])
    g_mlp = din("g_mlp", [1, DM])
    g_fin = din("g_final", [1, DM])
    g_q = din("g_q", [1, 128])
    g_k = din("g_k", [1, 128])
    cosT = din("cosT", [128, SEQ])
    sinT = din("sinT", [128, SEQ])
    abt = din("abt", [128, 12 * 256])
    kval = din("kval", [128, 72])
    rmat = din("rmat", [128, 128])
    y = nc.dram_tensor("y", [nseq * CH, DM], F32, kind="ExternalOutput").ap()
    skind = "ExternalOutput" if debug else "Internal"

    def dscr(name, shape):
        return nc.dram_tensor(name, shape, BF16, kind=skind).ap()

    wb_in = dscr("wb_in", [DM, IN_W])
    wb_br = dscr("wb_br", [DM, DM])
    wb_out = dscr("wb_out", [DM, DM])
    wb_ff1 = dscr("wb_ff1", [DM, DFF])
    wb_ff2 = dscr("wb_ff2", [DFF, DM])
    KT = dscr("KT", [nseq * 4, 128, SEQ])
    VG = dscr("VG", [nseq * 4, 128, 128, 128])
    AQ = dscr("AQ", [nseq * 12, 128, CH])
    AK = dscr("AK", [nseq * 12, 128, 4096])
    AV = dscr("AV", [nseq * 12, 128, 4096])
    BQ = dscr("BQ", [nseq * 12, 128, CH])
    YT = dscr("YT", [nseq * 16, 128, CH])
    D_ = {n: Buf(n) for n in ["wb_in", "wb_br", "wb_out", "wb_ff1", "wb_ff2", "KT", "VG", "AQ", "AK", "AV", "BQ", "YT", "y"]}

    P = Prog(nc)
    st = contextlib.ExitStack()
    with st:
        NW = 52480
        arena_t = st.enter_context(nc.sbuf_tensor("arena", [128, NW], F32))
        psum_t = st.enter_context(nc.psum_tensor("psum", [128, 4096], F32))
        A = Arena(arena_t[:, :], NW)
        PB = [T(psum_t[:, 512 * i:512 * (i + 1)], f"pb{i}") for i in range(8)]
        for t_ in PB:
            t_.b.excl = True

        def pb_bf(i):
            return PB[i].ap.bitcast(BF16).rearrange("p (a b) -> p a b", b=128)

        ident = A.take("ident", (128,), BF16)
        ones = A.take("ones", (128,), BF16)
        rm = A.take("rm", (128,), BF16)
        epsc = A.take("eps", (1,), F32)
        gcol = A.take("gcol", (4,), F32)
        gmixc = A.take("gmixc", (16,), F32)
        gmlpc = A.take("gmlpc", (16,), F32)
        kv = A.take("kv", (72,), F32)
        pm = A.mark()

        tmpf = A.take("tmpf", (128,), F32)
        P.op("pool", lambda e: e.memset(tmpf.ap, 0.0), writes=[tmpf.b])
        P.op("pool", lambda e: e.affine_select(out=tmpf.ap, in_=tmpf.ap, pattern=[[-1, 128]], compare_op=ALU.not_equal,
                                               fill=1.0, base=0, channel_multiplier=1), reads=[tmpf.b], writes=[tmpf.b])
        P.op("dve", lambda e: e.tensor_copy(out=ident.ap, in_=tmpf.ap), reads=[tmpf.b], writes=[ident.b])
        P.op("dve", lambda e: e.memset(ones.ap, 1.0), writes=[ones.b])
        P.op("dve", lambda e: e.memset(epsc.ap, EPS), writes=[epsc.b])
        tmpr = A.take("tmpr", (128,), F32)
        P.dma("sp", tmpr.ap, rmat[:, :], writes=[tmpr.b])
        P.op("dve", lambda e: e.tensor_copy(out=rm.ap, in_=tmpr.ap), reads=[tmpr.b], writes=[rm.b])
        NCD = dict(allow_slow_non_contiguous=True)
        P.dma("sp", gcol.ap[:, 0:1], g_q.rearrange("o d -> d o"), writes=[gcol.b], **NCD)
        P.dma("sp", gcol.ap[:, 2:3], g_k.rearrange("o d -> d o"), writes=[gcol.b], **NCD)
        for (lo, src) in ((0, 32), (32, 0), (64, 96), (96, 64)):
            P.dma("sp", gcol.ap[lo:lo + 32, 1:2], g_q[:, src:src + 32].rearrange("o d -> d o"), writes=[gcol.b], **NCD)
            P.dma("sp", gcol.ap[lo:lo + 32, 3:4], g_k[:, src:src + 32].rearrange("o d -> d o"), writes=[gcol.b], **NCD)
        for c in range(16):
            P.dma("sp", gmixc.ap[:, c:c + 1], g_mix[:, c * 128:(c + 1) * 128].rearrange("o d -> d o"), writes=[gmixc.b], **NCD)
            P.dma("sp", gmlpc.ap[:, c:c + 1], g_mlp[:, c * 128:(c + 1) * 128].rearrange("o d -> d o"), writes=[gmlpc.b], **NCD)
        P.dma("sp", kv.ap, kval[:, :], writes=[kv.b])

        def cast_copy(dst, src, rows, cols, dbuf):
            for r0 in range(0, rows, 1024):
                for c0 in range(0, cols, 2048):
                    P.dma("pool", dst[r0:r0 + 1024, c0:c0 + 2048], src[r0:r0 + 1024, c0:c0 + 2048], writes=[dbuf])
        def scaled_copy(dst, src, cols, gc, dbuf):
            XS = [A.take(f"xs{i}", (2048,), F32) for i in range(4)]
            HS = [A.take(f"hs{i}", (2048,), BF16) for i in range(4)]
            k = 0
            for c in range(16):
                for c0 in range(0, cols, 2048):
                    wd = min(2048, cols - c0)
                    xs, hs = XS[k % 4], HS[k % 4]
                    P.dma("sp", xs.ap[:, :wd], src[c * 128:(c + 1) * 128, c0:c0 + wd], writes=[xs.b])
                    if k % 2 == 0:
                        P.op("dve", lambda e, xs=xs, hs=hs, c=c, wd=wd: e.tensor_scalar(
                            out=hs.ap[:, :wd], in0=xs.ap[:, :wd], scalar1=gc.ap[:, c:c + 1], scalar2=None, op0=ALU.mult),
                            reads=[xs.b, gc.b], writes=[hs.b])
                    else:
                        P.op("act", lambda e, xs=xs, hs=hs, c=c, wd=wd: e.activation(
                            out=hs.ap[:, :wd], in_=xs.ap[:, :wd], func=AF.Identity, scale=gc.ap[:, c:c + 1]),
                            reads=[xs.b, gc.b], writes=[hs.b])
                    P.dma("sp", dst[c * 128:(c + 1) * 128, c0:c0 + wd], hs.ap[:, :wd], reads=[hs.b], writes=[dbuf])
                    k += 1

        m0 = A.mark()
        if "p" not in phases:
            scaled_copy(wb_in, w_in, IN_W, gmixc, D_["wb_in"])
            scaled_copy(wb_ff1, w_ff1, DFF, gmlpc, D_["wb_ff1"])
        if "c" not in phases:
            cast_copy(wb_br, w_br, DM, DM, D_["wb_br"])
            cast_copy(wb_out, w_out, DM, DM, D_["wb_out"])
            cast_copy(wb_ff2, w_ff2, DFF, DM, D_["wb_ff2"])
        P.barrier()
        A.reset(pm)

        def rms_front(load_src, XBs, HBs, HTt, stat, tp_banks, pfx):
            for j in range(4):
                xb, hb = XBs[j], HBs[j % len(HBs)]
                ss, sd, rs = stat[j]
                if load_src is not None:
                    P.dma("sp", xb.ap, load_src(j), writes=[xb.b])
                P.op("act", lambda e, xb=xb, hb=hb, ss=ss: e.activation(out=hb.ap, in_=xb.ap, func=AF.Square, accum_out=ss.ap),
                     reads=[xb.b], writes=[hb.b, ss.b])
                P.op("act", lambda e, ss=ss, sd=sd: e.activation(out=sd.ap, in_=ss.ap, func=AF.Ln, scale=1.0 / DM, bias=epsc.ap[:, 0:1]),
                     reads=[ss.b, epsc.b], writes=[sd.b])
                P.op("act", lambda e, sd=sd, rs=rs: e.activation(out=rs.ap, in_=sd.ap, func=AF.Exp, scale=-0.5), reads=[sd.b], writes=[rs.b])
                P.op("dve", lambda e, xb=xb, hb=hb, rs=rs: e.tensor_scalar(out=hb.ap, in0=xb.ap, scalar1=rs.ap[:, 0:1], scalar2=None, op0=ALU.mult),
                     reads=[xb.b, rs.b], writes=[hb.b])
                for half in range(2):
                    bk = tp_banks[half]
                    for c in range(8):
                        cc = half * 8 + c
                        P.op("pe", lambda e, bk=bk, c=c, cc=cc, hb=hb: e.transpose(out=pb_bf(bk)[:, c, :], in_=hb.ap[:, cc * 128:(cc + 1) * 128], identity=ident.ap),
                             reads=[hb.b, ident.b], writes=[PB[bk].b])
                    dst = HTt.ap[:, half * 8:(half + 1) * 8, j * 128:(j + 1) * 128]
                    if half == 0:
                        P.op("dve", lambda e, bk=bk, dst=dst: e.tensor_copy(out=dst, in_=pb_bf(bk)), reads=[PB[bk].b], writes=[HTt.b])
                    else:
                        P.op("act", lambda e, bk=bk, dst=dst: e.activation(out=dst, in_=pb_bf(bk), func=AF.Copy), reads=[PB[bk].b], writes=[HTt.b])

        def mk_stats(n, pfx):
            res = []
            for i in range(n):
                res.append(tuple(A.take(f"{pfx}{i}_{k}", (1,), F32) for k in range(3)))
            return res

        for s in range(nseq):
            if "A" in phases:
                A.reset(pm)
                Wkv = A.take("Wkv", (16, 1024), BF16)
                XB = [A.take(f"XB{i}", (2048,), F32) for i in range(4)]
                HB = [A.take(f"HB{i}", (2048,), BF16) for i in range(2)]
                HT = [A.take(f"HT{i}", (16, 512), BF16) for i in range(2)]
                WR = [A.take(f"WR{i}", (16, 512), BF16) for i in range(3)]
                OST = [A.take(f"OST{i}", (4, 512), BF16) for i in range(3)]
                VST = [A.take(f"VST{i}", (4, 512), BF16) for i in range(2)]
                CS = [(A.take(f"C{i}", (512,), F32), A.take(f"S{i}", (512,), F32)) for i in range(2)]
                RT = [dict(sq=A.take(f"sq{i}", (512,), BF16), qb=A.take(f"qb{i}", (512,), BF16), ln=A.take(f"ln{i}", (512,), F32),
                           t1=A.take(f"t1{i}", (512,), F32), t2=A.take(f"t2{i}", (512,), F32)) for i in range(2)]
                STAT = [mk_stats(4, f"stA{i}") for i in range(2)]
                P.dma("sp", Wkv.ap, wb_in[:, COL_BK:COL_BK + 1024].rearrange("(c p) n -> p c n", p=128), reads=[D_["wb_in"]], writes=[Wkv.b])

                cnt = dict(pj=0, rope=0, ost=0, vst=0, wr=0, ev=0)

                def front_ab(e):
                    rms_front(lambda j, e=e: xr[s * SEQ + e * 512 + j * 128: s * SEQ + e * 512 + (j + 1) * 128, :],
                              XB, HB, HT[e % 2], STAT[e % 2], (0, 1), "ab")
                    c_, s_ = CS[e % 2]
                    P.dma("sp", c_.ap, cosT[:, e * 512:(e + 1) * 512], writes=[c_.b])
                    P.dma("sp", s_.ap, sinT[:, e * 512:(e + 1) * 512], writes=[s_.b])

                def proj_ab(e):
                    ht = HT[e % 2]
                    c_, s_ = CS[e % 2]
                    blocks = []
                    blocks.append(("kgrp", None))
                    if e < 8:
                        own = 2 <= e <= 5
                        near = e in (1, 6)
                        groups = []
                        for g in range(3):
                            if own:
                                groups.append(("aq", g))
                            if own or near or g == 2:
                                groups.append(("ak", g))
                                groups.append(("av", g))
                        if own:
                            for i in range(3):
                                groups.append(("bq", i))
                        for grp in groups:
                            blocks.append(("sgrp", grp))
                    for kind, grp in blocks:
                        if kind == "kgrp":
                            wt, wcol0, rope, gi = Wkv, 0, ("norope" not in DBG), 2
                        else:
                            nm, g = grp
                            col0 = {"aq": COL_AQ, "ak": COL_AK, "av": COL_AV, "bq": COL_BQ}[nm] + g * 512
                            wt = WR[cnt["wr"] % 3]
                            cnt["wr"] += 1
                            P.dma("sp", wt.ap, wb_in[:, col0:col0 + 512].rearrange("(c p) n -> p c n", p=128), reads=[D_["wb_in"]], writes=[wt.b])
                            wcol0, rope, gi = 0, (nm == "bq" and "norope" not in DBG), 0
                        ost = OST[cnt["ost"] % 3]
                        cnt["ost"] += 1
                        pend = None
                        for hh in range(5):
                            if hh < 4:
                                bk = 2 + cnt["pj"] % 4
                                cnt["pj"] += 1
                                for c in range(16):
                                    P.op("pe", lambda e_, bk=bk, wt=wt, c=c, hh=hh, wcol0=wcol0, ht=ht: e_.matmul(
                                        PB[bk].ap, lhsT=wt.ap[:, c, wcol0 + hh * 128: wcol0 + (hh + 1) * 128], rhs=ht.ap[:, c, :], start=(c == 0), stop=(c == 15)),
                                        reads=[wt.b, ht.b], writes=[PB[bk].b])
                                if rope:
                                    rt = RT[cnt["rope"] % 2]
                                    sb_, qb_ = 6, 7
                                    cnt["rope"] += 1
                                    if ROPEN >= 1:
                                        P.op("act", lambda e_, bk=bk, rt=rt: e_.activation(out=rt["sq"].ap, in_=PB[bk].ap, func=AF.Square), reads=[PB[bk].b], writes=[rt["sq"].b])
                                    if ROPEN >= 2:
                                        P.op("dve", lambda e_, bk=bk, rt=rt: e_.tensor_copy(out=rt["qb"].ap, in_=PB[bk].ap), reads=[PB[bk].b], writes=[rt["qb"].b])
                                    cur = (bk, rt, sb_, qb_, hh)
                                else:
                                    dst = ost.ap[:, hh, :]
                                    if cnt["ev"] % 2 == 0 or "alldve" in DBG:
                                        P.op("dve", lambda e_, bk=bk, dst=dst: e_.tensor_copy(out=dst, in_=PB[bk].ap), reads=[PB[bk].b], writes=[ost.b])
                                    else:
                                        P.op("act", lambda e_, bk=bk, dst=dst: e_.activation(out=dst, in_=PB[bk].ap, func=AF.Copy), reads=[PB[bk].b], writes=[ost.b])
                                    cnt["ev"] += 1
                                    cur = None
                            else:
                                cur = None
                            if pend is not None:
                                bk0, rt, sb_, qb_, h0 = pend
                                if ROPEN >= 3:
                                    P.op("pe", lambda e_, sb_=sb_, rt=rt: e_.matmul(PB[sb_].ap, lhsT=ones.ap, rhs=rt["sq"].ap, start=True, stop=True),
                                         reads=[ones.b, rt["sq"].b], writes=[PB[sb_].b])
                                if ROPEN >= 4:
                                    P.op("pe", lambda e_, qb_=qb_, rt=rt: e_.matmul(PB[qb_].ap, lhsT=rm.ap, rhs=rt["qb"].ap, start=True, stop=True),
                                         reads=[rm.b, rt["qb"].b], writes=[PB[qb_].b])
                                if ROPEN >= 5:
                                    P.op("act", lambda e_, sb_=sb_, rt=rt: e_.activation(out=rt["ln"].ap, in_=PB[sb_].ap, func=AF.Ln, scale=1.0 / 128, bias=epsc.ap[:, 0:1]),
                                         reads=[PB[sb_].b, epsc.b], writes=[rt["ln"].b])
                                if ROPEN >= 6:
                                    P.op("act", lambda e_, rt=rt: e_.activation(out=rt["ln"].ap, in_=rt["ln"].ap, func=AF.Exp, scale=-0.5),
                                         reads=[rt["ln"].b], writes=[rt["ln"].b])
                                if ROPEN >= 7:
                                    P.op("dve", lambda e_, bk0=bk0, rt=rt, gi=gi: e_.scalar_tensor_tensor(out=rt["t1"].ap, in0=PB[bk0].ap, scalar=gcol.ap[:, gi:gi + 1], in1=c_.ap, op0=ALU.mult, op1=ALU.mult),
                                         reads=[PB[bk0].b, gcol.b, c_.b], writes=[rt["t1"].b])
                                if ROPEN >= 8:
                                    P.op("dve", lambda e_, qb_=qb_, rt=rt, gi=gi: e_.scalar_tensor_tensor(out=rt["t2"].ap, in0=PB[qb_].ap, scalar=gcol.ap[:, gi + 1:gi + 2], in1=s_.ap, op0=ALU.mult, op1=ALU.mult),
                                         reads=[PB[qb_].b, gcol.b, s_.b], writes=[rt["t2"].b])
                                if ROPEN >= 9:
                                    P.op("pool", lambda e_, rt=rt: e_.tensor_tensor(out=rt["t1"].ap, in0=rt["t1"].ap, in1=rt["t2"].ap, op=ALU.add),
                                         reads=[rt["t1"].b, rt["t2"].b], writes=[rt["t1"].b])
                                if ROPEN >= 10:
                                    P.op("dve", lambda e_, rt=rt, h0=h0, ost=ost: e_.tensor_tensor(out=ost.ap[:, h0, :], in0=rt["t1"].ap, in1=rt["ln"].ap, op=ALU.mult),
                                         reads=[rt["t1"].b, rt["ln"].b], writes=[ost.b])
                                else:
                                    P.op("dve", lambda e_, bk0=bk0, h0=h0, ost=ost: e_.tensor_copy(out=ost.ap[:, h0, :], in_=PB[bk0].ap), reads=[PB[bk0].b], writes=[ost.b])
                            pend = cur
                        if kind == "kgrp":
                          if "nostore" not in DBG:
                            P.dma("pool", KT[s * 4:(s + 1) * 4, :, e * 512:(e + 1) * 512].rearrange("h p n -> p h n"), ost.ap, reads=[ost.b], writes=[D_["KT"]])
                            vst = VST[cnt["vst"] % 2]
                            cnt["vst"] += 1
                            for j in range(4):
                                bk = 2 + cnt["pj"] % 4
                                cnt["pj"] += 1
                                for c in range(16):
                                    P.op("pe", lambda e_, bk=bk, c=c, j=j, ht=ht: e_.matmul(
                                        PB[bk].ap, lhsT=ht.ap[:, c, j * 128:(j + 1) * 128], rhs=Wkv.ap[:, c, 512:1024], start=(c == 0), stop=(c == 15)),
                                        reads=[Wkv.b, ht.b], writes=[PB[bk].b])
                                if j % 2 == 0:
                                    P.op("dve", lambda e_, bk=bk, j=j, vst=vst: e_.tensor_copy(out=vst.ap[:, j, :], in_=PB[bk].ap), reads=[PB[bk].b], writes=[vst.b])
                                else:
                                    P.op("act", lambda e_, bk=bk, j=j, vst=vst: e_.activation(out=vst.ap[:, j, :], in_=PB[bk].ap, func=AF.Copy), reads=[PB[bk].b], writes=[vst.b])
                            for kh in range(4 if "nostore" not in DBG else 0):
                                P.dma("pool", VG[s * 4 + kh, :, 4 * e:4 * e + 4, :], vst.ap[:, :, kh * 128:(kh + 1) * 128], reads=[vst.b], writes=[D_["VG"]])
                        else:
                            nm, g = grp
                            if nm == "aq":
                                dst = AQ[s * 12 + 4 * g: s * 12 + 4 * g + 4, :, (e - 2) * 512:(e - 1) * 512]
                                db = D_["AQ"]
                            elif nm == "ak":
                                dst = AK[s * 12 + 4 * g: s * 12 + 4 * g + 4, :, e * 512:(e + 1) * 512]
                                db = D_["AK"]
                            elif nm == "av":
                                dst = AV[s * 12 + 4 * g: s * 12 + 4 * g + 4, :, e * 512:(e + 1) * 512]
                                db = D_["AV"]
                            else:
                                dst = BQ[s * 12 + 4 * g: s * 12 + 4 * g + 4, :, (e - 2) * 512:(e - 1) * 512]
                                db = D_["BQ"]
                            if "nostore" not in DBG:
                                P.dma("pool", dst.rearrange("h p n -> p h n"), ost.ap, reads=[ost.b], writes=[db])

                NTL = ntl
                front_ab(0)
                for e in range(NTL):
                    if e + 1 < NTL:
                        front_ab(e + 1)
                    proj_ab(e)
                P.barrier()

            if "C" in phases:
                A.reset(pm)
                ABT = A.take("ABT", (12, 2, 128), F32)
                P.dma("sp", ABT.ap, abt.rearrange("p (h a q) -> p h a q", h=12, a=2), writes=[ABT.b])
                QTOK = [A.take(f"qtok{i}", (2048,), BF16) for i in range(2)]
                KTOK = [A.take(f"ktok{i}", (4096,), BF16) for i in range(2)]
                VTOK = [A.take(f"vtok{i}", (4096,), BF16) for i in range(2)]
                QR = A.take("qr", (2048,), BF16)
                KR = A.take("kr", (4096,), BF16)
                VR = A.take("vr", (4096,), BF16)
                VT = A.take("vt", (32, 128), BF16)
                SBT = [A.take(f"sbt{i}", (4, 2, 128), F32) for i in range(2)]
                PT = [A.take(f"ptc{i}", (4, 2, 128), BF16) for i in range(2)]
                NUM = A.take("num", (2048,), F32)
                DEN = A.take("den", (2048,), F32)
                YA = [A.take(f"ya{i}", (2048,), BF16) for i in range(2)]
                ucnt = 0
                bcnt = 0
                for hs in range(4):
                    for g in range(3):
                        d = DIL[g]
                        Lr = CH // d
                        H = 64 * d
                        Wd = Lr + 128
                        hd = 4 * g + hs
                        qt_, kt_, vt_ = QTOK[ucnt % 2], KTOK[ucnt % 2], VTOK[ucnt % 2]
                        ucnt += 1
                        P.dma("sp", qt_.ap, AQ[s * 12 + hd, :, :], reads=[D_["AQ"]], writes=[qt_.b])
                        P.dma("sp", kt_.ap[:, :d * Wd], AK[s * 12 + hd, :, 1024 - H:3072 + H], reads=[D_["AK"]], writes=[kt_.b])
                        P.dma("sp", vt_.ap[:, :d * Wd], AV[s * 12 + hd, :, 1024 - H:3072 + H], reads=[D_["AV"]], writes=[vt_.b])
                        qr3 = QR.ap.rearrange("p (r j) -> p r j", r=d)
                        kr3 = KR.ap[:, :d * Wd].rearrange("p (r j) -> p r j", r=d)
                        vr3 = VR.ap[:, :d * Wd].rearrange("p (r j) -> p r j", r=d)
                        P.op("dve", lambda e, qt_=qt_, qr3=qr3, d=d: e.tensor_copy(out=qr3, in_=qt_.ap.rearrange("p (j r) -> p r j", r=d)), reads=[qt_.b], writes=[QR.b])
                        P.op("pool", lambda e, kt_=kt_, kr3=kr3, d=d, Wd=Wd: e.tensor_copy(out=kr3, in_=kt_.ap[:, :d * Wd].rearrange("p (j r) -> p r j", r=d)), reads=[kt_.b], writes=[KR.b])
                        P.op("act", lambda e, vt_=vt_, vr3=vr3, d=d, Wd=Wd: e.activation(out=vr3, in_=vt_.ap[:, :d * Wd].rearrange("p (j r) -> p r j", r=d), func=AF.Copy), reads=[vt_.b], writes=[VR.b])
                        nper = Lr // 128 + 1
                        nkt = d * nper
                        for b0 in range(0, nkt, 8):
                            nb = min(8, nkt - b0)
                            bk = 0 if (b0 // 8) % 2 == 0 else 2
                            for i in range(nb):
                                kti = b0 + i
                                r, ii = kti // nper, kti % nper
                                P.op("pe", lambda e, bk=bk, i=i, r=r, ii=ii, vr3=vr3: e.transpose(out=pb_bf(bk)[:, i, :], in_=vr3[:, r, ii * 128:(ii + 1) * 128], identity=ident.ap),
                                     reads=[VR.b, ident.b], writes=[PB[bk].b])
                            P.op("dve", lambda e, bk=bk, b0=b0, nb=nb: e.tensor_copy(out=VT.ap[:, b0:b0 + nb, :], in_=pb_bf(bk)[:, :nb, :]), reads=[PB[bk].b], writes=[VT.b])
                        for b in range(4):
                            sl = bcnt % 2
                            bcnt += 1
                            sb0, sb1 = (0, 1) if sl == 0 else (2, 3)
                            ob, db = (4, 6) if sl == 0 else (5, 7)
                            sbt, pt = SBT[sl], PT[sl]
                            tiles = []
                            for tt in range(4):
                                t = 4 * b + tt
                                r = (128 * t) // Lr
                                qi = ((128 * t) % Lr) // 128
                                tiles.append((r, qi))
                                bank = sb0 if tt < 2 else sb1
                                for ab in range(2):
                                    col = ((tt % 2) * 2 + ab) * 128
                                    P.op("pe", lambda e, bank=bank, col=col, r=r, qi=qi, ab=ab, kr3=kr3, qr3=qr3: e.matmul(
                                        PB[bank].ap[:, col:col + 128], lhsT=kr3[:, r, (qi + ab) * 128:(qi + ab + 1) * 128], rhs=qr3[:, r, qi * 128:(qi + 1) * 128], start=True, stop=True),
                                        reads=[KR.b, QR.b], writes=[PB[bank].b])
                            for half in range(2):
                                bank = sb0 if half == 0 else sb1
                                P.op("dve", lambda e, bank=bank, half=half, sbt=sbt, hd=hd: e.scalar_tensor_tensor(
                                    out=sbt.ap[:, half * 2:half * 2 + 2, :, :], in0=PB[bank].ap.rearrange("p (t a q) -> p t a q", t=2, a=2), scalar=SCALE,
                                    in1=ABT.ap[:, hd, :, :].unsqueeze(1).to_broadcast([128, 2, 2, 128]), op0=ALU.mult, op1=ALU.add),
                                    reads=[PB[bank].b, ABT.b], writes=[sbt.b])
                            for tt in range(4):
                                r, qi = tiles[tt]
                                for ab in range(2):
                                    kti = KT_BASE[g] + r * nper + qi + ab
                                    P.op("act", lambda e, tt=tt, ab=ab, kti=kti, sbt=sbt, pt=pt: e.activation(out=pt.ap[:, tt, ab, :], in_=sbt.ap[:, tt, ab, :], func=AF.Exp, bias=kv.ap[:, kti:kti + 1]),
                                         reads=[sbt.b, kv.b], writes=[pt.b])
                            for tt in range(4):
                                r, qi = tiles[tt]
                                for ab in range(2):
                                    kl = r * nper + qi + ab
                                    P.op("pe", lambda e, ob=ob, tt=tt, ab=ab, kl=kl, pt=pt: e.matmul(PB[ob].ap[:, tt * 128:(tt + 1) * 128], lhsT=VT.ap[:, kl, :], rhs=pt.ap[:, tt, ab, :], start=(ab == 0), stop=(ab == 1)),
                                         reads=[VT.b, pt.b], writes=[PB[ob].b])
                                for ab in range(2):
                                    P.op("pe", lambda e, db=db, tt=tt, ab=ab, pt=pt: e.matmul(PB[db].ap[:, tt * 128:(tt + 1) * 128], lhsT=ones.ap, rhs=pt.ap[:, tt, ab, :], start=(ab == 0), stop=(ab == 1)),
                                         reads=[ones.b, pt.b], writes=[PB[db].b])
                            if d == 1:
                                r0, nr, j0, nj = 0, 1, 512 * b, 512
                            elif d == 4:
                                r0, nr, j0, nj = b, 1, 0, 512
                            else:
                                r0, nr, j0, nj = 4 * b, 4, 0, 128
                            nview = NUM.ap.rearrange("p (j r) -> p r j", r=d)[:, r0:r0 + nr, j0:j0 + nj]
                            dview = DEN.ap.rearrange("p (j r) -> p r j", r=d)[:, r0:r0 + nr, j0:j0 + nj]
                            osrc = PB[ob].ap.rearrange("p (r j) -> p r j", r=nr)
                            dsrc = PB[db].ap.rearrange("p (r j) -> p r j", r=nr)
                            if g == 0:
                                P.op("dve", lambda e, nview=nview, osrc=osrc: e.tensor_copy(out=nview, in_=osrc), reads=[PB[ob].b], writes=[NUM.b])
                                P.op("act", lambda e, dview=dview, dsrc=dsrc: e.activation(out=dview, in_=dsrc, func=AF.Copy), reads=[PB[db].b], writes=[DEN.b])
                            else:
                                P.op("dve", lambda e, nview=nview, osrc=osrc: e.tensor_tensor(out=nview, in0=osrc, in1=nview, op=ALU.add), reads=[PB[ob].b, NUM.b], writes=[NUM.b])
                                P.op("dve", lambda e, dview=dview, dsrc=dsrc: e.tensor_tensor(out=dview, in0=dsrc, in1=dview, op=ALU.add), reads=[PB[db].b, DEN.b], writes=[DEN.b])
                    ya = YA[hs % 2]
                    P.op("dve", lambda e: e.reciprocal(out=DEN.ap, in_=DEN.ap), reads=[DEN.b], writes=[DEN.b])
                    P.op("dve", lambda e, ya=ya: e.tensor_tensor(out=ya.ap, in0=NUM.ap, in1=DEN.ap, op=ALU.mult), reads=[NUM.b, DEN.b], writes=[ya.b])
                    P.dma("pool", YT[s * 16 + hs, :, :], ya.ap, reads=[ya.b], writes=[D_["YT"]])
                P.barrier()

            if "D" in phases:
                A.reset(pm)
                KH = [A.take(f"kh{i}", (8192,), BF16) for i in range(2)]
                VH = [A.take(f"vh{i}", (64, 128), BF16) for i in range(2)]
                QB_ = [A.take(f"qd{i}", (512,), BF16) for i in range(3)]
                NSL = 3
                PTD = [A.take(f"ptd{i}", (1024,), BF16) for i in range(NSL)]
                PS2 = [A.take(f"ps2{i}", (512,), BF16) for i in range(NSL)]
                OS = [A.take(f"os{i}", (512,), F32) for i in range(2)]
                DS = [A.take(f"ds{i}", (512,), F32) for i in range(2)]
                YS = [A.take(f"ys{i}", (512,), BF16) for i in range(2)]
                items = []
                blocks = [(kh, j, qt) for kh in range(4) for j in range(3) for qt in range(4)]
                for bi, (kh, j, qt) in enumerate(blocks):
                    for kp in range(64):
                        items.append((bi, kh, j, qt, kp))
                NI = len(items)
                OB, DB = 6, 7

                def stage_qk(it):
                    bi, kh, j, qt, kp = items[it]
                    if kp == 0:
                        if j == 0 and qt == 0:
                            for hf in range(2):
                                P.dma("sp", KH[hf].ap, KT[s * 4 + kh, :, hf * 8192:(hf + 1) * 8192], reads=[D_["KT"]], writes=[KH[hf].b])
                                P.dma("sp", VH[hf].ap, VG[s * 4 + kh, :, hf * 64:(hf + 1) * 64, :], reads=[D_["VG"]], writes=[VH[hf].b])
                        qb = QB_[bi % 3]
                        P.dma("sp", qb.ap, BQ[s * 12 + kh * 3 + j, :, qt * 512:(qt + 1) * 512], reads=[D_["BQ"]], writes=[qb.b])
                    qb = QB_[bi % 3]
                    sl = it % NSL
                    for hh in range(2):
                        ktile = 2 * kp + hh
                        hf, ko = ktile // 64, (ktile % 64) * 128
                        bank = 2 * sl + hh
                        P.op("pe", lambda e, bank=bank, hf=hf, ko=ko, qb=qb: e.matmul(PB[bank].ap, lhsT=KH[hf].ap[:, ko:ko + 128], rhs=qb.ap, start=True, stop=True),
                             reads=[KH[hf].b, qb.b], writes=[PB[bank].b])
                    pt = PTD[sl]
                    P.op("act", lambda e, sl=sl, pt=pt: e.activation(out=pt.ap, in_=psum_t[:, 1024 * sl:1024 * (sl + 1)], func=AF.Exp, scale=SCALE),
                         reads=[PB[2 * sl].b, PB[2 * sl + 1].b], writes=[pt.b])
                    ps2 = PS2[sl]
                    P.op("dve", lambda e, pt=pt, ps2=ps2: e.tensor_tensor(out=ps2.ap, in0=pt.ap[:, 0:512], in1=pt.ap[:, 512:1024], op=ALU.add),
                         reads=[pt.b], writes=[ps2.b])

                def stage_pv(it):
                    bi, kh, j, qt, kp = items[it]
                    sl = it % NSL
                    pt, ps2 = PTD[sl], PS2[sl]
                    for hh in range(2):
                        ktile = 2 * kp + hh
                        hf, kl = ktile // 64, ktile % 64
                        first = (kp == 0 and hh == 0)
                        last = (kp == 63 and hh == 1)
                        P.op("pe", lambda e, hf=hf, kl=kl, hh=hh, pt=pt, first=first, last=last: e.matmul(PB[OB].ap, lhsT=VH[hf].ap[:, kl, :], rhs=pt.ap[:, hh * 512:(hh + 1) * 512], start=first, stop=last),
                             reads=[VH[hf].b, pt.b], writes=[PB[OB].b])
                    P.op("pe", lambda e, ps2=ps2, kp=kp: e.matmul(PB[DB].ap, lhsT=ones.ap, rhs=ps2.ap, start=(kp == 0), stop=(kp == 63)),
                         reads=[ones.b, ps2.b], writes=[PB[DB].b])
                    if kp == 63:
                        os_, ds_, ys = OS[bi % 2], DS[bi % 2], YS[bi % 2]
                        P.op("dve", lambda e, ds_=ds_: e.tensor_copy(out=ds_.ap, in_=PB[DB].ap), reads=[PB[DB].b], writes=[ds_.b])
                        P.op("dve", lambda e, os_=os_: e.tensor_copy(out=os_.ap, in_=PB[OB].ap), reads=[PB[OB].b], writes=[os_.b])
                        P.op("dve", lambda e, ds_=ds_: e.reciprocal(out=ds_.ap, in_=ds_.ap), reads=[ds_.b], writes=[ds_.b])
                        P.op("dve", lambda e, os_=os_, ds_=ds_, ys=ys: e.tensor_tensor(out=ys.ap, in0=os_.ap, in1=ds_.ap, op=ALU.mult), reads=[os_.b, ds_.b], writes=[ys.b])
                        P.dma("pool", YT[s * 16 + 4 + kh * 3 + j, :, qt * 512:(qt + 1) * 512], ys.ap, reads=[ys.b], writes=[D_["YT"]])

                LK = 2
                for it in range(NI + LK):
                    if it < NI:
                        stage_qk(it)
                    if it >= LK:
                        stage_pv(it - LK)
                P.barrier()

            if "E" in phases:
                A.reset(pm)
                GF = A.take("gf", (2048,), F32)
                P.dma("sp", GF.ap, g_fin[0:1, :].to_broadcast([128, DM]), writes=[GF.b])
                X1 = [A.take(f"x1_{i}", (2048,), F32) for i in range(4)]
                HBE = [A.take(f"hbe{i}", (2048,), BF16) for i in range(2)]
                HTE = A.take("hte", (16, 512), BF16)
                YTL = A.take("ytl", (16, 512), BF16)
                MG = A.take("mg", (16, 512), BF16)
                UT = A.take("ut", (32, 512), BF16)
                WE = [A.take(f"we{i}", (16, 512), BF16) for i in range(3)]
                SG = [(A.take(f"sga{i}", (512,), F32), A.take(f"sgb{i}", (512,), F32), A.take(f"tm{i}", (512,), F32)) for i in range(2)]
                RL = [A.take(f"rl{i}", (512,), F32) for i in range(2)]
                OUTS = [A.take(f"outs{i}", (2048,), F32) for i in range(1)]
                STE = mk_stats(4, "stE")
                wcnt = [0]

                def wload(src_ap, dbuf):
                    wt = WE[wcnt[0] % 3]
                    wcnt[0] += 1
                    P.dma("sp", wt.ap, src_ap, reads=[dbuf], writes=[wt.b])
                    return wt

                for ti in range(4):
                    e = 2 + ti
                    tok0 = ti * 512
                    rms_front(lambda j, e=e: xr[s * SEQ + e * 512 + j * 128: s * SEQ + e * 512 + (j + 1) * 128, :],
                              X1, HBE, HTE, STE, (0, 1), "e1")
                    P.dma("sp", YTL.ap, YT[s * 16:(s + 1) * 16, :, tok0:tok0 + 512].rearrange("c p n -> p c n"), reads=[D_["YT"]], writes=[YTL.b])
                    gcnt = 0
                    for fg in range(4):
                        wga = wload(wb_in[:, COL_GA + fg * 512: COL_GA + (fg + 1) * 512].rearrange("(c p) n -> p c n", p=128), D_["wb_in"])
                        wgb = wload(wb_in[:, COL_GB + fg * 512: COL_GB + (fg + 1) * 512].rearrange("(c p) n -> p c n", p=128), D_["wb_in"])
                        wbr = wload(wb_br[:, fg * 512:(fg + 1) * 512].rearrange("(c p) n -> p c n", p=128), D_["wb_br"])
                        for fb in range(4):
                            f = fg * 4 + fb
                            base = 4 * (gcnt % 2)
                            sga, sgb, tm = SG[gcnt % 2]
                            gcnt += 1
                            bga, bgb, boa, bob = base, base + 1, base + 2, base + 3
                            for c in range(16):
                                P.op("pe", lambda e_, c=c, fb=fb, wga=wga, bga=bga: e_.matmul(PB[bga].ap, lhsT=wga.ap[:, c, fb * 128:(fb + 1) * 128], rhs=HTE.ap[:, c, :], start=(c == 0), stop=(c == 15)),
                                     reads=[wga.b, HTE.b], writes=[PB[bga].b])
                            for c in range(16):
                                P.op("pe", lambda e_, c=c, fb=fb, wgb=wgb, bgb=bgb: e_.matmul(PB[bgb].ap, lhsT=wgb.ap[:, c, fb * 128:(fb + 1) * 128], rhs=HTE.ap[:, c, :], start=(c == 0), stop=(c == 15)),
                                     reads=[wgb.b, HTE.b], writes=[PB[bgb].b])
                            for c in range(4):
                                P.op("pe", lambda e_, c=c, fb=fb, wbr=wbr, boa=boa: e_.matmul(PB[boa].ap, lhsT=wbr.ap[:, c, fb * 128:(fb + 1) * 128], rhs=YTL.ap[:, c, :], start=(c == 0), stop=(c == 3)),
                                     reads=[wbr.b, YTL.b], writes=[PB[boa].b])
                            for c in range(4, 16):
                                P.op("pe", lambda e_, c=c, fb=fb, wbr=wbr, bob=bob: e_.matmul(PB[bob].ap, lhsT=wbr.ap[:, c, fb * 128:(fb + 1) * 128], rhs=YTL.ap[:, c, :], start=(c == 4), stop=(c == 15)),
                                     reads=[wbr.b, YTL.b], writes=[PB[bob].b])
                            P.op("act", lambda e_, bga=bga, sga=sga: e_.activation(out=sga.ap, in_=PB[bga].ap, func=AF.Sigmoid), reads=[PB[bga].b], writes=[sga.b])
                            P.op("act", lambda e_, bgb=bgb, sgb=sgb: e_.activation(out=sgb.ap, in_=PB[bgb].ap, func=AF.Sigmoid), reads=[PB[bgb].b], writes=[sgb.b])
                            P.op("dve", lambda e_, boa=boa, sga=sga, tm=tm: e_.tensor_tensor(out=tm.ap, in0=PB[boa].ap, in1=sga.ap, op=ALU.mult), reads=[PB[boa].b, sga.b], writes=[tm.b])
                            P.op("dve", lambda e_, bob=bob, sgb=sgb: e_.tensor_tensor(out=sgb.ap, in0=PB[bob].ap, in1=sgb.ap, op=ALU.mult), reads=[PB[bob].b, sgb.b], writes=[sgb.b])
                            P.op("pool", lambda e_, f=f, tm=tm, sgb=sgb: e_.tensor_tensor(out=MG.ap[:, f, :], in0=tm.ap, in1=sgb.ap, op=ALU.add), reads=[tm.b, sgb.b], writes=[MG.b])
                    pcnt = 0
                    for cb in range(4):
                        wo = wload(wb_out[:, cb * 512:(cb + 1) * 512].rearrange("(c p) n -> p c n", p=128), D_["wb_out"])
                        for sub in range(4):
                            bk = pcnt % 4
                            pcnt += 1
                            for c in range(16):
                                P.op("pe", lambda e_, c=c, sub=sub, wo=wo, bk=bk: e_.matmul(PB[bk].ap, lhsT=MG.ap[:, c, sub * 128:(sub + 1) * 128], rhs=wo.ap[:, c, :], start=(c == 0), stop=(c == 15)),
                                     reads=[MG.b, wo.b], writes=[PB[bk].b])
                            xs = X1[sub]
                            P.op("dve", lambda e_, bk=bk, xs=xs, cb=cb: e_.tensor_tensor(out=xs.ap[:, cb * 512:(cb + 1) * 512], in0=PB[bk].ap, in1=xs.ap[:, cb * 512:(cb + 1) * 512], op=ALU.add),
                                 reads=[PB[bk].b, xs.b], writes=[xs.b])
                    rms_front(None, X1, HBE, HTE, STE, (4, 5), "e4")
                    for hf in range(2):
                        ecnt = 0
                        for fg in range(8):
                            col0 = hf * 4096 + fg * 512
                            w1 = wload(wb_ff1[:, col0:col0 + 512].rearrange("(c p) n -> p c n", p=128), D_["wb_ff1"])
                            for fb in range(4):
                                bk = ecnt % 4
                                rl = RL[ecnt % 2]
                                ecnt += 1
                                for c in range(16):
                                    P.op("pe", lambda e_, c=c, fb=fb, w1=w1, bk=bk: e_.matmul(PB[bk].ap, lhsT=w1.ap[:, c, fb * 128:(fb + 1) * 128], rhs=HTE.ap[:, c, :], start=(c == 0), stop=(c == 15)),
                                         reads=[w1.b, HTE.b], writes=[PB[bk].b])
                                P.op("act", lambda e_, bk=bk, rl=rl: e_.activation(out=rl.ap, in_=PB[bk].ap, func=AF.Relu), reads=[PB[bk].b], writes=[rl.b])
                                ub = fg * 4 + fb
                                if ecnt % 2 == 0:
                                    P.op("dve", lambda e_, rl=rl, ub=ub: e_.tensor_tensor(out=UT.ap[:, ub, :], in0=rl.ap, in1=rl.ap, op=ALU.mult), reads=[rl.b], writes=[UT.b])
                                else:
                                    P.op("pool", lambda e_, rl=rl, ub=ub: e_.tensor_tensor(out=UT.ap[:, ub, :], in0=rl.ap, in1=rl.ap, op=ALU.mult), reads=[rl.b], writes=[UT.b])
                        for cb in range(4):
                            w2 = []
                            for kq in range(2):
                                r0 = hf * 4096 + kq * 2048
                                w2.append(wload(wb_ff2[r0:r0 + 2048, cb * 512:(cb + 1) * 512].rearrange("(c p) n -> p c n", p=128), D_["wb_ff2"]))
                            for sub in range(4):
                                bk = 4 + sub
                                for kq in range(2):
                                    for c in range(16):
                                        uc = kq * 16 + c
                                        P.op("pe", lambda e_, c=c, uc=uc, sub=sub, kq=kq, w2=w2, bk=bk: e_.matmul(PB[bk].ap, lhsT=UT.ap[:, uc, sub * 128:(sub + 1) * 128], rhs=w2[kq].ap[:, c, :], start=(uc == 0), stop=(uc == 31)),
                                             reads=[UT.b, w2[kq].b], writes=[PB[bk].b])
                                xs = X1[sub]
                                P.op("dve", lambda e_, bk=bk, xs=xs, cb=cb: e_.tensor_tensor(out=xs.ap[:, cb * 512:(cb + 1) * 512], in0=PB[bk].ap, in1=xs.ap[:, cb * 512:(cb + 1) * 512], op=ALU.add),
                                     reads=[PB[bk].b, xs.b], writes=[xs.b])
                    for sub in range(4):
                        xs = X1[sub]
                        ss, sd, rs = STE[sub]
                        outs = OUTS[0]
                        P.op("act", lambda e_, xs=xs, outs=outs, ss=ss: e_.activation(out=outs.ap, in_=xs.ap, func=AF.Square, accum_out=ss.ap), reads=[xs.b], writes=[outs.b, ss.b])
                        P.op("act", lambda e_, ss=ss, sd=sd: e_.activation(out=sd.ap, in_=ss.ap, func=AF.Sqrt, scale=1.0 / DM, bias=epsc.ap[:, 0:1]), reads=[ss.b, epsc.b], writes=[sd.b])
                        P.op("dve", lambda e_, sd=sd, rs=rs: e_.reciprocal(out=rs.ap, in_=sd.ap), reads=[sd.b], writes=[rs.b])
                        P.op("dve", lambda e_, xs=xs, outs=outs, rs=rs: e_.scalar_tensor_tensor(out=outs.ap, in0=xs.ap, scalar=rs.ap[:, 0:1], in1=GF.ap, op0=ALU.mult, op1=ALU.mult),
                             reads=[xs.b, rs.b, GF.b], writes=[outs.b])
                        r0 = s * CH + tok0 + sub * 128
                        P.dma("pool", y[r0:r0 + 128, :], outs.ap, reads=[outs.b], writes=[D_["y"]])
                P.barrier()

        P.emit_all([])
    return nc


def _const_tables():
    n = 12
    slopes = 2.0 ** (-8.0 * np.arange(1, n + 1, dtype=np.float64) / n)
    i = np.arange(128)[:, None]
    j = np.arange(128)[None, :]
    ab = np.zeros((128, 12, 2, 128), np.float32)
    for g in range(3):
        for hs in range(4):
            hd = 4 * g + hs
            for a, rel in enumerate((i - 64 - j, i + 64 - j)):
                v = -slopes[hd] * DIL[g] * np.abs(rel)
                ab[:, hd, a, :] = np.where(np.abs(rel) <= 64, v, NEG)
    rmat = np.zeros((128, 128), np.float32)
    for m in range(128):
        pr = m + 32 if (m % 64) < 32 else m - 32
        rmat[pr, m] = 1.0
    return ab.reshape(128, 12 * 256), rmat


def _rope_tables(c):
    u = np.arange(SEQ)
    t = (c * CH - 1024 + u) % SEQ
    row = (t // 64).astype(np.float32)
    col = (t % 64).astype(np.float32)
    half = 64
    inv = (10000.0 ** (-np.arange(0, half, 2, dtype=np.float32) / half)).astype(np.float32)
    ang_r = row[None, :] * inv[:, None]
    ang_c = col[None, :] * inv[:, None]
    cosT = np.concatenate([np.cos(ang_r), np.cos(ang_r), np.cos(ang_c), np.cos(ang_c)], axis=0).astype(np.float32)
    sinT = np.concatenate([-np.sin(ang_r), np.sin(ang_r), -np.sin(ang_c), np.sin(ang_c)], axis=0).astype(np.float32)
    return np.ascontiguousarray(cosT), np.ascontiguousarray(sinT)


def _kvalid(c):
    kvd = np.zeros((128, 72), np.float32)
    m = np.arange(128)
    for g in range(3):
        d = DIL[g]
        Lr = CH // d
        nper = Lr // 128 + 1
        for r in range(d):
            for i in range(nper):
                t = c * CH - 64 * d + (128 * i + m) * d + r
                kvd[:, KT_BASE[g] + r * nper + i] = np.where((t >= 0) & (t < SEQ), 0.0, NEG)
    return kvd


_NC_CACHE = {}


def kernel(x_prompt, x_sample, g_mix, w_in, g_q, g_k, w_branch, w_out, g_mlp, w_ff1, w_ff2, g_final):
    f = lambda a: np.ascontiguousarray(np.asarray(a, dtype=np.float32))
    seqs = [f(x_prompt)[0], f(x_sample)[0], f(x_sample)[1]]
    if "nc" not in _NC_CACHE:
        _NC_CACHE["nc"] = build()
    nc = _NC_CACHE["nc"]
    ab, rmat = _const_tables()
    common = {
        "w_in": f(w_in)[0], "w_branch": f(w_branch)[0], "w_out": f(w_out)[0], "w_ff1": f(w_ff1)[0], "w_ff2": f(w_ff2)[0],
        "g_mix": f(g_mix).reshape(1, DM), "g_mlp": f(g_mlp).reshape(1, DM), "g_final": f(g_final).reshape(1, DM),
        "g_q": f(g_q).reshape(1, 128), "g_k": f(g_k).reshape(1, 128), "abt": ab, "rmat": rmat,
    }
    in_maps = []
    for c in range(NCORE):
        sh = c * CH - 1024
        xr = np.concatenate([np.roll(sq, -sh, axis=0) for sq in seqs], axis=0)
        cosT, sinT = _rope_tables(c)
        m = dict(common)
        m.update({"xr": xr, "cosT": cosT, "sinT": sinT, "kval": _kvalid(c)})
        in_maps.append(m)
    res = run_bass_kernel_spmd(nc, in_maps, core_ids=list(range(NCORE)))
    outs = [np.empty((SEQ, DM), np.float32) for _ in range(3)]
    for c in range(NCORE):
        yc = res.results[c]["y"]
        for s in range(3):
            outs[s][c * CH:(c + 1) * CH] = yc[s * CH:(s + 1) * CH]
    y_prompt = outs[0][None]
    y_sample = np.stack([outs[1], outs[2]], axis=0)
    return (y_prompt, y_sample)
```

```python
import contextlib
import math
import numpy as np
import concourse.bass as bass
import concourse.mybir as mybir
from concourse.bass_utils import run_bass_kernel_spmd

F32 = mybir.dt.float32
BF16 = mybir.dt.bfloat16
AF = mybir.ActivationFunctionType
ALU = mybir.AluOpType

SEQ = 16384
DM = 2048
NCORE = 8
CH = 2048
NSEQ = 3
DIL = (1, 4, 16)
COL_AQ, COL_AK, COL_AV, COL_BQ, COL_BK, COL_GA, COL_GB = 0, 1536, 3072, 4608, 6144, 7168, 9216
IN_W = 11264
DFF = 8192
EPS = 1e-6
NEG = -30000.0
SCALE = 128 ** -0.5
KT_BASE = (0, 17, 37)
NKT = 69

COMPUTE = ("pe", "act", "dve", "pool")
EPOCH = 12000


class Buf:
    __slots__ = ("name", "w", "r", "excl")

    def __init__(self, name="", excl=False):
        self.name = name
        self.w = None
        self.r = []
        self.excl = excl


class Op:
    __slots__ = ("eng", "emit", "deps", "marked", "dma", "sem", "val", "slot_prev")

    def __init__(self, eng, emit, dma):
        self.eng = eng
        self.emit = emit
        self.deps = []
        self.marked = False
        self.dma = dma
        self.sem = None
        self.val = 0
        self.slot_prev = None


class Prog:
    def __init__(self, nc, dma_ring=16):
        self.nc = nc
        self.ops = {e: [] for e in ("pe", "act", "dve", "pool", "sp")}
        self.dma_ring = dma_ring
        self.dmas = []

    def op(self, eng, emit, reads=(), writes=(), dma=False):
        o = Op(eng, emit, dma)
        deps = {}
        writes = list(writes) + [b for b in reads if b.excl]
        reads = [b for b in reads if not b.excl]
        for b in reads:
            if b.w is not None:
                deps[id(b.w)] = (b.w, "raw")
        for b in writes:
            if b.w is not None and id(b.w) not in deps:
                deps[id(b.w)] = (b.w, "waw")
            for r in b.r:
                if id(r) not in deps:
                    deps[id(r)] = (r, "war")
        for d, kind in deps.values():
            if d is o:
                continue
            if (not dma) and (not d.dma) and d.eng == eng:
                if eng == "pe" or kind != "raw":
                    continue
            o.deps.append(d)
            d.marked = True
        for b in reads:
            b.r.append(o)
        for b in writes:
            b.w = o
            b.r = []
        self.ops[eng].append(o)
        if dma:
            self.dmas.append(o)
        return o

    def dma(self, q, out, in_, reads=(), writes=(), **kw):
        return self.op(q, lambda e: e.dma_start(out=out, in_=in_, **kw), reads, writes, dma=True)

    def barrier(self):
        lasts = []
        for e in COMPUTE:
            for o in reversed(self.ops[e]):
                if not o.dma and o.emit is not None:
                    lasts.append(o)
                    break
        pend = lasts + self.dmas
        self.dmas = []
        for e in ("act", "dve", "pool", "sp"):
            o = Op(e, None, False)
            for d in pend:
                if d.eng == e and not d.dma:
                    continue
                o.deps.append(d)
                d.marked = True
            self.ops[e].append(o)

    def emit_all(self, final_ops):
        nc = self.nc
        with contextlib.ExitStack() as st:
            eng_sems = {}
            for e in COMPUTE:
                cnt = 0
                ep = 0
                sems = [st.enter_context(nc.semaphore(f"s_{e}_0"))]
                for o in self.ops[e]:
                    if o.dma or not o.marked or o.emit is None:
                        continue
                    if cnt >= EPOCH:
                        ep += 1
                        cnt = 0
                        sems.append(st.enter_context(nc.semaphore(f"s_{e}_{ep}")))
                    cnt += 1
                    o.sem = (e, ep)
                    o.val = cnt
                eng_sems[e] = sems
            dma_sems = {}
            for q in ("sp", "act", "pool"):
                dl = [o for o in self.ops[q] if o.dma]
                if not dl:
                    continue
                nslot = min(self.dma_ring, len(dl))
                sems = [st.enter_context(nc.semaphore(f"d_{q}_{i}")) for i in range(nslot)]
                last = [None] * nslot
                vals = [0] * nslot
                for i, o in enumerate(dl):
                    s = i % nslot
                    vals[s] += 16
                    o.sem = (q + "_dma", s)
                    o.val = vals[s]
                    o.slot_prev = last[s]
                    last[s] = o
                dma_sems[q] = sems

            def sem_of(o):
                k, i = o.sem
                if k.endswith("_dma"):
                    return dma_sems[k[:-4]][i]
                return eng_sems[k][i]

            block = st.enter_context(nc.Block())
            handles = {"pe": block.tensor, "act": block.scalar, "dve": block.vector,
                       "pool": block.gpsimd, "sp": block.sync}

            def make(ename):
                ops = self.ops[ename]
                finals = final_ops if ename == "sp" else []

                def body(eng):
                    waited = {}

                    def wait_for(d):
                        k, i = d.sem
                        if k.endswith("_dma"):
                            key = (k, i)
                            if waited.get(key, 0) >= d.val:
                                return
                            waited[key] = d.val
                        else:
                            cur = waited.get(k, (-1, 0))
                            if cur >= (i, d.val):
                                return
                            waited[k] = (i, d.val)
                        eng.wait_ge(sem_of(d), d.val)

                    for o in ops:
                        if o.dma and o.slot_prev is not None:
                            wait_for(o.slot_prev)
                        for d in o.deps:
                            wait_for(d)
                        if o.emit is None:
                            continue
                        ins = o.emit(eng)
                        if o.dma:
                            ins.then_inc(sem_of(o), 16)
                        elif o.marked:
                            ins.then_inc(sem_of(o), 1)
                    for d in finals:
                        wait_for(d)
                return body

            for ename in ("sp", "act", "dve", "pool", "pe"):
                handles[ename](make(ename))


class T:
    __slots__ = ("ap", "b")

    def __init__(self, ap, name=""):
        self.ap = ap
        self.b = Buf(name)


class Arena:
    def __init__(self, base_ap, nwords):
        self.base = base_ap
        self.n = nwords
        self.top = 0

    def mark(self):
        return self.top

    def reset(self, m):
        self.top = m

    def take(self, name, free_shape, dtype):
        esz = 4 if dtype == F32 else 2
        nel = int(np.prod(free_shape))
        nw = (nel * esz + 3) // 4
        nw = (nw + 7) // 8 * 8
        off = self.top
        self.top += nw
        assert self.top <= self.n, f"arena overflow at {name}: {self.top*4} > {self.n*4}"
        v = self.base[:, off:off + (nel * esz) // 4]
        if dtype != F32:
            v = v.bitcast(dtype)
        if len(free_shape) > 1:
            names = [f"a{i}" for i in range(len(free_shape))]
            pat = "p (" + " ".join(names) + ") -> p " + " ".join(names)
            kw = {n: int(s) for n, s in zip(names[1:], free_shape[1:])}
            v = v.rearrange(pat, **kw)
        return T(v, name)


def build(nseq=NSEQ, debug=False, phases="ABCDE", ntl=SEQ // 512):
    nc = bass.Bass("TRN2", target_bir_lowering=False)
    import os
    DBG = os.environ.get("K_DBG", "").split(",")
    ROPEN = int(os.environ.get("K_ROPEN", "99"))

    def din(name, shape):
        return nc.dram_tensor(name, shape, F32, kind="ExternalInput").ap()

    xr = din("xr", [nseq * SEQ, DM])
    w_in = din("w_in", [DM, IN_W])
    w_br = din("w_branch", [DM, DM])
    w_out = din("w_out", [DM, DM])
    w_ff1 = din("w_ff1", [DM, DFF])
    w_ff2 = din("w_ff2", [DFF, DM])
    g_mix = din("g_mix", [1, DM])
    g_mlp = din("g_mlp", [1, DM])
    g_fin = din("g_final", [1, DM])
    g_q = din("g_q", [1, 128])
    g_k = din("g_k", [1, 128])
    cosT = din("cosT", [128, SEQ])
    sinT = din("sinT", [128, SEQ])
    abt = din("abt", [128, 12 * 256])
    kval = din("kval", [128, 72])
    rmat = din("rmat", [128, 128])
    y = nc.dram_tensor("y", [nseq * CH, DM], F32, kind="ExternalOutput").ap()
    skind = "ExternalOutput" if debug else "Internal"

    def dscr(name, shape):
        return nc.dram_tensor(name, shape, BF16, kind=skind).ap()

    wb_in = dscr("wb_in", [DM, IN_W])
    wb_br = dscr("wb_br", [DM, DM])
    wb_out = dscr("wb_out", [DM, DM])
    wb_ff1 = dscr("wb_ff1", [DM, DFF])
    wb_ff2 = dscr("wb_ff2", [DFF, DM])
    KT = dscr("KT", [nseq * 4, 128, SEQ])
    VG = dscr("VG", [nseq * 4, 128, 128, 128])
    AQ = dscr("AQ", [nseq * 12, 128, CH])
    AK = dscr("AK", [nseq * 12, 128, 4096])
    AV = dscr("AV", [nseq * 12, 128, 4096])
    BQ = dscr("BQ", [nseq * 12, 128, CH])
    YT = dscr("YT", [nseq * 16, 128, CH])
    D_ = {n: Buf(n) for n in ["wb_in", "wb_br", "wb_out", "wb_ff1", "wb_ff2", "KT", "VG", "AQ", "AK", "AV", "BQ", "YT", "y"]}

    P = Prog(nc)
    st = contextlib.ExitStack()
    with st:
        NW = 52480
        arena_t = st.enter_context(nc.sbuf_tensor("arena", [128, NW], F32))
        psum_t = st.enter_context(nc.psum_tensor("psum", [128, 4096], F32))
        A = Arena(arena_t[:, :], NW)
        PB = [T(psum_t[:, 512 * i:512 * (i + 1)], f"pb{i}") for i in range(8)]
        for t_ in PB:
            t_.b.excl = True

        def pb_bf(i):
            return PB[i].ap.bitcast(BF16).rearrange("p (a b) -> p a b", b=128)

        ident = A.take("ident", (128,), BF16)
        ones = A.take("ones", (128,), BF16)
        rm = A.take("rm", (128,), BF16)
        epsc = A.take("eps", (1,), F32)
        gcol = A.take("gcol", (4,), F32)
        gmixc = A.take("gmixc", (16,), F32)
        gmlpc = A.take("gmlpc", (16,), F32)
        kv = A.take("kv", (72,), F32)
        pm = A.mark()

        tmpf = A.take("tmpf", (128,), F32)
        P.op("pool", lambda e: e.memset(tmpf.ap, 0.0), writes=[tmpf.b])
        P.op("pool", lambda e: e.affine_select(out=tmpf.ap, in_=tmpf.ap, pattern=[[-1, 128]], compare_op=ALU.not_equal,
                                               fill=1.0, base=0, channel_multiplier=1), reads=[tmpf.b], writes=[tmpf.b])
        P.op("dve", lambda e: e.tensor_copy(out=ident.ap, in_=tmpf.ap), reads=[tmpf.b], writes=[ident.b])
        P.op("dve", lambda e: e.memset(ones.ap, 1.0), writes=[ones.b])
        P.op("dve", lambda e: e.memset(epsc.ap, EPS), writes=[epsc.b])
        tmpr = A.take("tmpr", (128,), F32)
        P.dma("sp", tmpr.ap, rmat[:, :], writes=[tmpr.b])
        P.op("dve", lambda e: e.tensor_copy(out=rm.ap, in_=tmpr.ap), reads=[tmpr.b], writes=[rm.b])
        NCD = dict(allow_slow_non_contiguous=True)
        P.dma("sp", gcol.ap[:, 0:1], g_q.rearrange("o d -> d o"), writes=[gcol.b], **NCD)
        P.dma("sp", gcol.ap[:, 2:3], g_k.rearrange("o d -> d o"), writes=[gcol.b], **NCD)
        for (lo, src) in ((0, 32), (32, 0), (64, 96), (96, 64)):
            P.dma("sp", gcol.ap[lo:lo + 32, 1:2], g_q[:, src:src + 32].rearrange("o d -> d o"), writes=[gcol.b], **NCD)
            P.dma("sp", gcol.ap[lo:lo + 32, 3:4], g_k[:, src:src + 32].rearrange("o d -> d o"), writes=[gcol.b], **NCD)
        for c in range(16):
            P.dma("sp", gmixc.ap[:, c:c + 1], g_mix[:, c * 128:(c + 1) * 128].rearrange("o d -> d o"), writes=[gmixc.b], **NCD)
            P.dma("sp", gmlpc.ap[:, c:c + 1], g_mlp[:, c * 128:(c + 1) * 128].rearrange("o d -> d o"), writes=[gmlpc.b], **NCD)
        P.dma("sp", kv.ap, kval[:, :], writes=[kv.b])

        def cast_copy(dst, src, rows, cols, dbuf):
            for r0 in range(0, rows, 1024):
                for c0 in range(0, cols, 2048):
                    P.dma("pool", dst[r0:r0 + 1024, c0:c0 + 2048], src[r0:r0 + 1024, c0:c0 + 2048], writes=[dbuf])
        def scaled_copy(dst, src, cols, gc, dbuf):
            NB_ = 4
            XS = [A.take(f"xs{i}", (2048,), F32) for i in range(NB_)]
            HS = [A.take(f"hs{i}", (2048,), BF16) for i in range(NB_)]
            pieces = [(c, c0, min(2048, cols - c0)) for c in range(16) for c0 in range(0, cols, 2048)]

            def load(k):
                c, c0, wd = pieces[k]
                xs = XS[k % NB_]
                P.dma("sp", xs.ap[:, :wd], src[c * 128:(c + 1) * 128, c0:c0 + wd], writes=[xs.b])

            LA = NB_ - 1
            for k in range(min(LA, len(pieces))):
                load(k)
            for k, (c, c0, wd) in enumerate(pieces):
                if k + LA < len(pieces):
                    load(k + LA)
                xs, hs = XS[k % NB_], HS[k % NB_]
                if k % 2 == 0:
                    P.op("dve", lambda e, xs=xs, hs=hs, c=c, wd=wd: e.tensor_scalar(
                        out=hs.ap[:, :wd], in0=xs.ap[:, :wd], scalar1=gc.ap[:, c:c + 1], scalar2=None, op0=ALU.mult),
                        reads=[xs.b, gc.b], writes=[hs.b])
                else:
                    P.op("act", lambda e, xs=xs, hs=hs, c=c, wd=wd: e.activation(
                        out=hs.ap[:, :wd], in_=xs.ap[:, :wd], func=AF.Identity, scale=gc.ap[:, c:c + 1]),
                        reads=[xs.b, gc.b], writes=[hs.b])
                P.dma("sp", dst[c * 128:(c + 1) * 128, c0:c0 + wd], hs.ap[:, :wd], reads=[hs.b], writes=[dbuf])

        m0 = A.mark()
        if "p" not in phases:
            scaled_copy(wb_in, w_in, IN_W, gmixc, D_["wb_in"])
            scaled_copy(wb_ff1, w_ff1, DFF, gmlpc, D_["wb_ff1"])
        if "c" not in phases:
            cast_copy(wb_br, w_br, DM, DM, D_["wb_br"])
            cast_copy(wb_out, w_out, DM, DM, D_["wb_out"])
            cast_copy(wb_ff2, w_ff2, DFF, DM, D_["wb_ff2"])
        P.barrier()
        A.reset(pm)

        def rms_front(load_src, XBs, HBs, HTt, stat, tp_banks, pfx):
            for j in range(4):
                xb, hb = XBs[j], HBs[j % len(HBs)]
                ss, sd, rs = stat[j]
                if load_src is not None:
                    P.dma("sp", xb.ap, load_src(j), writes=[xb.b])
                P.op("act", lambda e, xb=xb, hb=hb, ss=ss: e.activation(out=hb.ap, in_=xb.ap, func=AF.Square, accum_out=ss.ap),
                     reads=[xb.b], writes=[hb.b, ss.b])
                P.op("act", lambda e, ss=ss, sd=sd: e.activation(out=sd.ap, in_=ss.ap, func=AF.Ln, scale=1.0 / DM, bias=epsc.ap[:, 0:1]),
                     reads=[ss.b, epsc.b], writes=[sd.b])
                P.op("act", lambda e, sd=sd, rs=rs: e.activation(out=rs.ap, in_=sd.ap, func=AF.Exp, scale=-0.5), reads=[sd.b], writes=[rs.b])
                P.op("dve", lambda e, xb=xb, hb=hb, rs=rs: e.tensor_scalar(out=hb.ap, in0=xb.ap, scalar1=rs.ap[:, 0:1], scalar2=None, op0=ALU.mult),
                     reads=[xb.b, rs.b], writes=[hb.b])
                for half in range(2):
                    bk = tp_banks[half]
                    for c in range(8):
                        cc = half * 8 + c
                        P.op("pe", lambda e, bk=bk, c=c, cc=cc, hb=hb: e.transpose(out=pb_bf(bk)[:, c, :], in_=hb.ap[:, cc * 128:(cc + 1) * 128], identity=ident.ap),
                             reads=[hb.b, ident.b], writes=[PB[bk].b])
                    dst = HTt.ap[:, half * 8:(half + 1) * 8, j * 128:(j + 1) * 128]
                    if half == 0:
                        P.op("dve", lambda e, bk=bk, dst=dst: e.tensor_copy(out=dst, in_=pb_bf(bk)), reads=[PB[bk].b], writes=[HTt.b])
                    else:
                        P.op("act", lambda e, bk=bk, dst=dst: e.activation(out=dst, in_=pb_bf(bk), func=AF.Copy), reads=[PB[bk].b], writes=[HTt.b])

        def mk_stats(n, pfx):
            res = []
            for i in range(n):
                res.append(tuple(A.take(f"{pfx}{i}_{k}", (1,), F32) for k in range(3)))
            return res

        for s in range(nseq):
            if "A" in phases:
                A.reset(pm)
                Wkv = A.take("Wkv", (16, 1024), BF16)
                XB = [A.take(f"XB{i}", (2048,), F32) for i in range(4)]
                HB = [A.take(f"HB{i}", (2048,), BF16) for i in range(2)]
                HT = [A.take(f"HT{i}", (16, 512), BF16) for i in range(2)]
                WR = [A.take(f"WR{i}", (16, 512), BF16) for i in range(3)]
                OST = [A.take(f"OST{i}", (4, 512), BF16) for i in range(3)]
                VST = [A.take(f"VST{i}", (4, 512), BF16) for i in range(2)]
                CS = [(A.take(f"C{i}", (512,), F32), A.take(f"S{i}", (512,), F32)) for i in range(2)]
                RT = [dict(sq=A.take(f"sq{i}", (512,), BF16), qb=A.take(f"qb{i}", (512,), BF16), ln=A.take(f"ln{i}", (512,), F32),
                           t1=A.take(f"t1{i}", (512,), F32), t2=A.take(f"t2{i}", (512,), F32)) for i in range(2)]
                STAT = [mk_stats(4, f"stA{i}") for i in range(2)]
                P.dma("sp", Wkv.ap, wb_in[:, COL_BK:COL_BK + 1024].rearrange("(c p) n -> p c n", p=128), reads=[D_["wb_in"]], writes=[Wkv.b])

                cnt = dict(pj=0, rope=0, ost=0, vst=0, wr=0, ev=0)

                def front_ab(e):
                    rms_front(lambda j, e=e: xr[s * SEQ + e * 512 + j * 128: s * SEQ + e * 512 + (j + 1) * 128, :],
                              XB, HB, HT[e % 2], STAT[e % 2], (0, 1), "ab")
                    c_, s_ = CS[e % 2]
                    P.dma("sp", c_.ap, cosT[:, e * 512:(e + 1) * 512], writes=[c_.b])
                    P.dma("sp", s_.ap, sinT[:, e * 512:(e + 1) * 512], writes=[s_.b])

                def proj_ab(e):
                    ht = HT[e % 2]
                    c_, s_ = CS[e % 2]
                    blocks = []
                    blocks.append(("kgrp", None))
                    if e < 8:
                        own = 2 <= e <= 5
                        near = e in (1, 6)
                        groups = []
                        for g in range(3):
                            if own:
                                groups.append(("aq", g))
                            if own or near or g == 2:
                                groups.append(("ak", g))
                                groups.append(("av", g))
                        if own:
                            for i in range(3):
                                groups.append(("bq", i))
                        for grp in groups:
                            blocks.append(("sgrp", grp))
                    for kind, grp in blocks:
                        if kind == "kgrp":
                            wt, wcol0, rope, gi = Wkv, 0, ("norope" not in DBG), 2
                        else:
                            nm, g = grp
                            col0 = {"aq": COL_AQ, "ak": COL_AK, "av": COL_AV, "bq": COL_BQ}[nm] + g * 512
                            wt = WR[cnt["wr"] % 3]
                            cnt["wr"] += 1
                            P.dma("sp", wt.ap, wb_in[:, col0:col0 + 512].rearrange("(c p) n -> p c n", p=128), reads=[D_["wb_in"]], writes=[wt.b])
                            wcol0, rope, gi = 0, (nm == "bq" and "norope" not in DBG), 0
                        ost = OST[cnt["ost"] % 3]
                        cnt["ost"] += 1
                        pend = None
                        for hh in range(5):
                            if hh < 4:
                                bk = 2 + cnt["pj"] % 4
                                cnt["pj"] += 1
                                for c in range(16):
                                    P.op("pe", lambda e_, bk=bk, wt=wt, c=c, hh=hh, wcol0=wcol0, ht=ht: e_.matmul(
                                        PB[bk].ap, lhsT=wt.ap[:, c, wcol0 + hh * 128: wcol0 + (hh + 1) * 128], rhs=ht.ap[:, c, :], start=(c == 0), stop=(c == 15)),
                                        reads=[wt.b, ht.b], writes=[PB[bk].b])
                                if rope:
                                    rt = RT[cnt["rope"] % 2]
                                    sb_, qb_ = 6, 7
                                    cnt["rope"] += 1
                                    if ROPEN >= 1:
                                        P.op("act", lambda e_, bk=bk, rt=rt: e_.activation(out=rt["sq"].ap, in_=PB[bk].ap, func=AF.Square), reads=[PB[bk].b], writes=[rt["sq"].b])
                                    if ROPEN >= 2:
                                        P.op("dve", lambda e_, bk=bk, rt=rt: e_.tensor_copy(out=rt["qb"].ap, in_=PB[bk].ap), reads=[PB[bk].b], writes=[rt["qb"].b])
                                    cur = (bk, rt, sb_, qb_, hh)
                                else:
                                    dst = ost.ap[:, hh, :]
                                    if cnt["ev"] % 2 == 0 or "alldve" in DBG:
                                        P.op("dve", lambda e_, bk=bk, dst=dst: e_.tensor_copy(out=dst, in_=PB[bk].ap), reads=[PB[bk].b], writes=[ost.b])
                                    else:
                                        P.op("act", lambda e_, bk=bk, dst=dst: e_.activation(out=dst, in_=PB[bk].ap, func=AF.Copy), reads=[PB[bk].b], writes=[ost.b])
                                    cnt["ev"] += 1
                                    cur = None
                            else:
                                cur = None
                            if pend is not None:
                                bk0, rt, sb_, qb_, h0 = pend
                                if ROPEN >= 3:
                                    P.op("pe", lambda e_, sb_=sb_, rt=rt: e_.matmul(PB[sb_].ap, lhsT=ones.ap, rhs=rt["sq"].ap, start=True, stop=True),
                                         reads=[ones.b, rt["sq"].b], writes=[PB[sb_].b])
                                if ROPEN >= 4:
                                    P.op("pe", lambda e_, qb_=qb_, rt=rt: e_.matmul(PB[qb_].ap, lhsT=rm.ap, rhs=rt["qb"].ap, start=True, stop=True),
                                         reads=[rm.b, rt["qb"].b], writes=[PB[qb_].b])
                                if ROPEN >= 5:
                                    P.op("act", lambda e_, sb_=sb_, rt=rt: e_.activation(out=rt["ln"].ap, in_=PB[sb_].ap, func=AF.Ln, scale=1.0 / 128, bias=epsc.ap[:, 0:1]),
                                         reads=[PB[sb_].b, epsc.b], writes=[rt["ln"].b])
                                if ROPEN >= 6:
                                    P.op("act", lambda e_, rt=rt: e_.activation(out=rt["ln"].ap, in_=rt["ln"].ap, func=AF.Exp, scale=-0.5),
                                         reads=[rt["ln"].b], writes=[rt["ln"].b])
                                if ROPEN >= 7:
                                    P.op("dve", lambda e_, bk0=bk0, rt=rt, gi=gi: e_.scalar_tensor_tensor(out=rt["t1"].ap, in0=PB[bk0].ap, scalar=gcol.ap[:, gi:gi + 1], in1=c_.ap, op0=ALU.mult, op1=ALU.mult),
                                         reads=[PB[bk0].b, gcol.b, c_.b], writes=[rt["t1"].b])
                                if ROPEN >= 8:
                                    P.op("dve", lambda e_, qb_=qb_, rt=rt, gi=gi: e_.scalar_tensor_tensor(out=rt["t2"].ap, in0=PB[qb_].ap, scalar=gcol.ap[:, gi + 1:gi + 2], in1=s_.ap, op0=ALU.mult, op1=ALU.mult),
                                         reads=[PB[qb_].b, gcol.b, s_.b], writes=[rt["t2"].b])
                                if ROPEN >= 9:
                                    P.op("pool", lambda e_, rt=rt: e_.tensor_tensor(out=rt["t1"].ap, in0=rt["t1"].ap, in1=rt["t2"].ap, op=ALU.add),
                                         reads=[rt["t1"].b, rt["t2"].b], writes=[rt["t1"].b])
                                if ROPEN >= 10:
                                    P.op("dve", lambda e_, rt=rt, h0=h0, ost=ost: e_.tensor_tensor(out=ost.ap[:, h0, :], in0=rt["t1"].ap, in1=rt["ln"].ap, op=ALU.mult),
                                         reads=[rt["t1"].b, rt["ln"].b], writes=[ost.b])
                                else:
                                    P.op("dve", lambda e_, bk0=bk0, h0=h0, ost=ost: e_.tensor_copy(out=ost.ap[:, h0, :], in_=PB[bk0].ap), reads=[PB[bk0].b], writes=[ost.b])
                            pend = cur
                        if kind == "kgrp":
                          if "nostore" not in DBG:
                            P.dma("pool", KT[s * 4:(s + 1) * 4, :, e * 512:(e + 1) * 512].rearrange("h p n -> p h n"), ost.ap, reads=[ost.b], writes=[D_["KT"]])
                            vst = VST[cnt["vst"] % 2]
                            cnt["vst"] += 1
                            for j in range(4):
                                bk = 2 + cnt["pj"] % 4
                                cnt["pj"] += 1
                                for c in range(16):
                                    P.op("pe", lambda e_, bk=bk, c=c, j=j, ht=ht: e_.matmul(
                                        PB[bk].ap, lhsT=ht.ap[:, c, j * 128:(j + 1) * 128], rhs=Wkv.ap[:, c, 512:1024], start=(c == 0), stop=(c == 15)),
                                        reads=[Wkv.b, ht.b], writes=[PB[bk].b])
                                if j % 2 == 0:
                                    P.op("dve", lambda e_, bk=bk, j=j, vst=vst: e_.tensor_copy(out=vst.ap[:, j, :], in_=PB[bk].ap), reads=[PB[bk].b], writes=[vst.b])
                                else:
                                    P.op("act", lambda e_, bk=bk, j=j, vst=vst: e_.activation(out=vst.ap[:, j, :], in_=PB[bk].ap, func=AF.Copy), reads=[PB[bk].b], writes=[vst.b])
                            for kh in range(4 if "nostore" not in DBG else 0):
                                P.dma("pool", VG[s * 4 + kh, :, 4 * e:4 * e + 4, :], vst.ap[:, :, kh * 128:(kh + 1) * 128], reads=[vst.b], writes=[D_["VG"]])
                        else:
                            nm, g = grp
                            if nm == "aq":
                                dst = AQ[s * 12 + 4 * g: s * 12 + 4 * g + 4, :, (e - 2) * 512:(e - 1) * 512]
                                db = D_["AQ"]
                            elif nm == "ak":
                                dst = AK[s * 12 + 4 * g: s * 12 + 4 * g + 4, :, e * 512:(e + 1) * 512]
                                db = D_["AK"]
                            elif nm == "av":
                                dst = AV[s * 12 + 4 * g: s * 12 + 4 * g + 4, :, e * 512:(e + 1) * 512]
                                db = D_["AV"]
                            else:
                                dst = BQ[s * 12 + 4 * g: s * 12 + 4 * g + 4, :, (e - 2) * 512:(e - 1) * 512]
                                db = D_["BQ"]
                            if "nostore" not in DBG:
                                P.dma("pool", dst.rearrange("h p n -> p h n"), ost.ap, reads=[ost.b], writes=[db])

                NTL = ntl
                front_ab(0)
                for e in range(NTL):
                    if e + 1 < NTL:
                        front_ab(e + 1)
                    proj_ab(e)
                P.barrier()

            if "C" in phases:
                A.reset(pm)
                ABT = A.take("ABT", (12, 2, 128), F32)
                P.dma("sp", ABT.ap, abt.rearrange("p (h a q) -> p h a q", h=12, a=2), writes=[ABT.b])
                QTOK = [A.take(f"qtok{i}", (2048,), BF16) for i in range(2)]
                KTOK = [A.take(f"ktok{i}", (4096,), BF16) for i in range(2)]
                VTOK = [A.take(f"vtok{i}", (4096,), BF16) for i in range(2)]
                QR = A.take("qr", (2048,), BF16)
                KR = A.take("kr", (4096,), BF16)
                VR = A.take("vr", (4096,), BF16)
                VT = A.take("vt", (32, 128), BF16)
                SBT = [A.take(f"sbt{i}", (4, 2, 128), F32) for i in range(2)]
                PT = [A.take(f"ptc{i}", (4, 2, 128), BF16) for i in range(2)]
                NUM = A.take("num", (2048,), F32)
                DEN = A.take("den", (2048,), F32)
                YA = [A.take(f"ya{i}", (2048,), BF16) for i in range(2)]
                ucnt = 0
                bcnt = 0
                for hs in range(4):
                    for g in range(3):
                        d = DIL[g]
                        Lr = CH // d
                        H = 64 * d
                        Wd = Lr + 128
                        hd = 4 * g + hs
                        qt_, kt_, vt_ = QTOK[ucnt % 2], KTOK[ucnt % 2], VTOK[ucnt % 2]
                        ucnt += 1
                        P.dma("sp", qt_.ap, AQ[s * 12 + hd, :, :], reads=[D_["AQ"]], writes=[qt_.b])
                        P.dma("sp", kt_.ap[:, :d * Wd], AK[s * 12 + hd, :, 1024 - H:3072 + H], reads=[D_["AK"]], writes=[kt_.b])
                        P.dma("sp", vt_.ap[:, :d * Wd], AV[s * 12 + hd, :, 1024 - H:3072 + H], reads=[D_["AV"]], writes=[vt_.b])
                        qr3 = QR.ap.rearrange("p (r j) -> p r j", r=d)
                        kr3 = KR.ap[:, :d * Wd].rearrange("p (r j) -> p r j", r=d)
                        vr3 = VR.ap[:, :d * Wd].rearrange("p (r j) -> p r j", r=d)
                        P.op("dve", lambda e, qt_=qt_, qr3=qr3, d=d: e.tensor_copy(out=qr3, in_=qt_.ap.rearrange("p (j r) -> p r j", r=d)), reads=[qt_.b], writes=[QR.b])
                        P.op("pool", lambda e, kt_=kt_, kr3=kr3, d=d, Wd=Wd: e.tensor_copy(out=kr3, in_=kt_.ap[:, :d * Wd].rearrange("p (j r) -> p r j", r=d)), reads=[kt_.b], writes=[KR.b])
                        P.op("act", lambda e, vt_=vt_, vr3=vr3, d=d, Wd=Wd: e.activation(out=vr3, in_=vt_.ap[:, :d * Wd].rearrange("p (j r) -> p r j", r=d), func=AF.Copy), reads=[vt_.b], writes=[VR.b])
                        nper = Lr // 128 + 1
                        nkt = d * nper
                        for b0 in range(0, nkt, 8):
                            nb = min(8, nkt - b0)
                            bk = 0 if (b0 // 8) % 2 == 0 else 2
                            for i in range(nb):
                                kti = b0 + i
                                r, ii = kti // nper, kti % nper
                                P.op("pe", lambda e, bk=bk, i=i, r=r, ii=ii, vr3=vr3: e.transpose(out=pb_bf(bk)[:, i, :], in_=vr3[:, r, ii * 128:(ii + 1) * 128], identity=ident.ap),
                                     reads=[VR.b, ident.b], writes=[PB[bk].b])
                            P.op("dve", lambda e, bk=bk, b0=b0, nb=nb: e.tensor_copy(out=VT.ap[:, b0:b0 + nb, :], in_=pb_bf(bk)[:, :nb, :]), reads=[PB[bk].b], writes=[VT.b])
                        for b in range(4):
                            sl = bcnt % 2
                            bcnt += 1
                            sb0, sb1 = (0, 1) if sl == 0 else (2, 3)
                            ob, db = (4, 6) if sl == 0 else (5, 7)
                            sbt, pt = SBT[sl], PT[sl]
                            tiles = []
                            for tt in range(4):
                                t = 4 * b + tt
                                r = (128 * t) // Lr
                                qi = ((128 * t) % Lr) // 128
                                tiles.append((r, qi))
                                bank = sb0 if tt < 2 else sb1
                                for ab in range(2):
                                    col = ((tt % 2) * 2 + ab) * 128
                                    P.op("pe", lambda e, bank=bank, col=col, r=r, qi=qi, ab=ab, kr3=kr3, qr3=qr3: e.matmul(
                                        PB[bank].ap[:, col:col + 128], lhsT=kr3[:, r, (qi + ab) * 128:(qi + ab + 1) * 128], rhs=qr3[:, r, qi * 128:(qi + 1) * 128], start=True, stop=True),
                                        reads=[KR.b, QR.b], writes=[PB[bank].b])
                            for half in range(2):
                                bank = sb0 if half == 0 else sb1
                                P.op("dve", lambda e, bank=bank, half=half, sbt=sbt, hd=hd: e.scalar_tensor_tensor(
                                    out=sbt.ap[:, half * 2:half * 2 + 2, :, :], in0=PB[bank].ap.rearrange("p (t a q) -> p t a q", t=2, a=2), scalar=SCALE,
                                    in1=ABT.ap[:, hd, :, :].unsqueeze(1).to_broadcast([128, 2, 2, 128]), op0=ALU.mult, op1=ALU.add),
                                    reads=[PB[bank].b, ABT.b], writes=[sbt.b])
                            for tt in range(4):
                                r, qi = tiles[tt]
                                for ab in range(2):
                                    kti = KT_BASE[g] + r * nper + qi + ab
                                    P.op("act", lambda e, tt=tt, ab=ab, kti=kti, sbt=sbt, pt=pt: e.activation(out=pt.ap[:, tt, ab, :], in_=sbt.ap[:, tt, ab, :], func=AF.Exp, bias=kv.ap[:, kti:kti + 1]),
                                         reads=[sbt.b, kv.b], writes=[pt.b])
                            for tt in range(4):
                                r, qi = tiles[tt]
                                for ab in range(2):
                                    kl = r * nper + qi + ab
                                    P.op("pe", lambda e, ob=ob, tt=tt, ab=ab, kl=kl, pt=pt: e.matmul(PB[ob].ap[:, tt * 128:(tt + 1) * 128], lhsT=VT.ap[:, kl, :], rhs=pt.ap[:, tt, ab, :], start=(ab == 0), stop=(ab == 1)),
                                         reads=[VT.b, pt.b], writes=[PB[ob].b])
                                for ab in range(2):
                                    P.op("pe", lambda e, db=db, tt=tt, ab=ab, pt=pt: e.matmul(PB[db].ap[:, tt * 128:(tt + 1) * 128], lhsT=ones.ap, rhs=pt.ap[:, tt, ab, :], start=(ab == 0), stop=(ab == 1)),
                                         reads=[ones.b, pt.b], writes=[PB[db].b])
                            if d == 1:
                                r0, nr, j0, nj = 0, 1, 512 * b, 512
                            elif d == 4:
                                r0, nr, j0, nj = b, 1, 0, 512
                            else:
                                r0, nr, j0, nj = 4 * b, 4, 0, 128
                            nview = NUM.ap.rearrange("p (j r) -> p r j", r=d)[:, r0:r0 + nr, j0:j0 + nj]
                            dview = DEN.ap.rearrange("p (j r) -> p r j", r=d)[:, r0:r0 + nr, j0:j0 + nj]
                            osrc = PB[ob].ap.rearrange("p (r j) -> p r j", r=nr)
                            dsrc = PB[db].ap.rearrange("p (r j) -> p r j", r=nr)
                            if g == 0:
                                P.op("dve", lambda e, nview=nview, osrc=osrc: e.tensor_copy(out=nview, in_=osrc), reads=[PB[ob].b], writes=[NUM.b])
                                P.op("act", lambda e, dview=dview, dsrc=dsrc: e.activation(out=dview, in_=dsrc, func=AF.Copy), reads=[PB[db].b], writes=[DEN.b])
                            else:
                                P.op("dve", lambda e, nview=nview, osrc=osrc: e.tensor_tensor(out=nview, in0=osrc, in1=nview, op=ALU.add), reads=[PB[ob].b, NUM.b], writes=[NUM.b])
                                P.op("dve", lambda e, dview=dview, dsrc=dsrc: e.tensor_tensor(out=dview, in0=dsrc, in1=dview, op=ALU.add), reads=[PB[db].b, DEN.b], writes=[DEN.b])
                    ya = YA[hs % 2]
                    P.op("dve", lambda e: e.reciprocal(out=DEN.ap, in_=DEN.ap), reads=[DEN.b], writes=[DEN.b])
                    P.op("dve", lambda e, ya=ya: e.tensor_tensor(out=ya.ap, in0=NUM.ap, in1=DEN.ap, op=ALU.mult), reads=[NUM.b, DEN.b], writes=[ya.b])
                    P.dma("pool", YT[s * 16 + hs, :, :], ya.ap, reads=[ya.b], writes=[D_["YT"]])
                P.barrier()

            if "D" in phases:
                A.reset(pm)
                KH = [A.take(f"kh{i}", (8192,), BF16) for i in range(2)]
                VH = [A.take(f"vh{i}", (64, 128), BF16) for i in range(2)]
                QB_ = [A.take(f"qd{i}", (512,), BF16) for i in range(3)]
                NSL = 3
                PTD = [A.take(f"ptd{i}", (1024,), BF16) for i in range(NSL)]
                PS2 = [A.take(f"ps2{i}", (512,), BF16) for i in range(NSL)]
                OS = [A.take(f"os{i}", (512,), F32) for i in range(2)]
                DS = [A.take(f"ds{i}", (512,), F32) for i in range(2)]
                YS = [A.take(f"ys{i}", (512,), BF16) for i in range(2)]
                items = []
                blocks = [(kh, j, qt) for kh in range(4) for j in range(3) for qt in range(4)]
                for bi, (kh, j, qt) in enumerate(blocks):
                    for kp in range(64):
                        items.append((bi, kh, j, qt, kp))
                NI = len(items)
                OB, DB = 6, 7

                def stage_qk(it):
                    bi, kh, j, qt, kp = items[it]
                    if kp == 0:
                        if j == 0 and qt == 0:
                            for hf in range(2):
                                P.dma("sp", KH[hf].ap, KT[s * 4 + kh, :, hf * 8192:(hf + 1) * 8192], reads=[D_["KT"]], writes=[KH[hf].b])
                                P.dma("sp", VH[hf].ap, VG[s * 4 + kh, :, hf * 64:(hf + 1) * 64, :], reads=[D_["VG"]], writes=[VH[hf].b])
                        qb = QB_[bi % 3]
                        P.dma("sp", qb.ap, BQ[s * 12 + kh * 3 + j, :, qt * 512:(qt + 1) * 512], reads=[D_["BQ"]], writes=[qb.b])
                    qb = QB_[bi % 3]
                    sl = it % NSL
                    for hh in range(2):
                        ktile = 2 * kp + hh
                        hf, ko = ktile // 64, (ktile % 64) * 128
                        bank = 2 * sl + hh
                        P.op("pe", lambda e, bank=bank, hf=hf, ko=ko, qb=qb: e.matmul(PB[bank].ap, lhsT=KH[hf].ap[:, ko:ko + 128], rhs=qb.ap, start=True, stop=True),
                             reads=[KH[hf].b, qb.b], writes=[PB[bank].b])
                    pt = PTD[sl]
                    P.op("act", lambda e, sl=sl, pt=pt: e.activation(out=pt.ap, in_=psum_t[:, 1024 * sl:1024 * (sl + 1)], func=AF.Exp, scale=SCALE),
                         reads=[PB[2 * sl].b, PB[2 * sl + 1].b], writes=[pt.b])
                    ps2 = PS2[sl]
                    P.op("dve", lambda e, pt=pt, ps2=ps2: e.tensor_tensor(out=ps2.ap, in0=pt.ap[:, 0:512], in1=pt.ap[:, 512:1024], op=ALU.add),
                         reads=[pt.b], writes=[ps2.b])

                def stage_pv(it):
                    bi, kh, j, qt, kp = items[it]
                    sl = it % NSL
                    pt, ps2 = PTD[sl], PS2[sl]
                    for hh in range(2):
                        ktile = 2 * kp + hh
                        hf, kl = ktile // 64, ktile % 64
                        first = (kp == 0 and hh == 0)
                        last = (kp == 63 and hh == 1)
                        P.op("pe", lambda e, hf=hf, kl=kl, hh=hh, pt=pt, first=first, last=last: e.matmul(PB[OB].ap, lhsT=VH[hf].ap[:, kl, :], rhs=pt.ap[:, hh * 512:(hh + 1) * 512], start=first, stop=last),
                             reads=[VH[hf].b, pt.b], writes=[PB[OB].b])
                    P.op("pe", lambda e, ps2=ps2, kp=kp: e.matmul(PB[DB].ap, lhsT=ones.ap, rhs=ps2.ap, start=(kp == 0), stop=(kp == 63)),
                         reads=[ones.b, ps2.b], writes=[PB[DB].b])
                    if kp == 63:
                        os_, ds_, ys = OS[bi % 2], DS[bi % 2], YS[bi % 2]
                        P.op("dve", lambda e, ds_=ds_: e.tensor_copy(out=ds_.ap, in_=PB[DB].ap), reads=[PB[DB].b], writes=[ds_.b])
                        P.op("dve", lambda e, os_=os_: e.tensor_copy(out=os_.ap, in_=PB[OB].ap), reads=[PB[OB].b], writes=[os_.b])
                        P.op("dve", lambda e, ds_=ds_: e.reciprocal(out=ds_.ap, in_=ds_.ap), reads=[ds_.b], writes=[ds_.b])
                        P.op("dve", lambda e, os_=os_, ds_=ds_, ys=ys: e.tensor_tensor(out=ys.ap, in0=os_.ap, in1=ds_.ap, op=ALU.mult), reads=[os_.b, ds_.b], writes=[ys.b])
                        P.dma("pool", YT[s * 16 + 4 + kh * 3 + j, :, qt * 512:(qt + 1) * 512], ys.ap, reads=[ys.b], writes=[D_["YT"]])

                LK = 2
                for it in range(NI + LK):
                    if it < NI:
                        stage_qk(it)
                    if it >= LK:
                        stage_pv(it - LK)
                P.barrier()

            if "E" in phases:
                A.reset(pm)
                GF = A.take("gf", (2048,), F32)
                P.dma("sp", GF.ap, g_fin[0:1, :].to_broadcast([128, DM]), writes=[GF.b])
                X1 = [A.take(f"x1_{i}", (2048,), F32) for i in range(4)]
                HBE = [A.take(f"hbe{i}", (2048,), BF16) for i in range(2)]
                HTE = A.take("hte", (16, 512), BF16)
                YTL = A.take("ytl", (16, 512), BF16)
                MG = A.take("mg", (16, 512), BF16)
                UT = A.take("ut", (32, 512), BF16)
                WE = [A.take(f"we{i}", (16, 512), BF16) for i in range(3)]
                SG = [(A.take(f"sga{i}", (512,), F32), A.take(f"sgb{i}", (512,), F32), A.take(f"tm{i}", (512,), F32)) for i in range(2)]
                RL = [A.take(f"rl{i}", (512,), F32) for i in range(2)]
                OUTS = [A.take(f"outs{i}", (2048,), F32) for i in range(1)]
                STE = mk_stats(4, "stE")
                wcnt = [0]

                def wload(src_ap, dbuf):
                    wt = WE[wcnt[0] % 3]
                    wcnt[0] += 1
                    P.dma("sp", wt.ap, src_ap, reads=[dbuf], writes=[wt.b])
                    return wt

                for ti in range(4):
                    e = 2 + ti
                    tok0 = ti * 512
                    rms_front(lambda j, e=e: xr[s * SEQ + e * 512 + j * 128: s * SEQ + e * 512 + (j + 1) * 128, :],
                              X1, HBE, HTE, STE, (0, 1), "e1")
                    P.dma("sp", YTL.ap, YT[s * 16:(s + 1) * 16, :, tok0:tok0 + 512].rearrange("c p n -> p c n"), reads=[D_["YT"]], writes=[YTL.b])
                    gcnt = 0
                    for fg in range(4):
                        wga = wload(wb_in[:, COL_GA + fg * 512: COL_GA + (fg + 1) * 512].rearrange("(c p) n -> p c n", p=128), D_["wb_in"])
                        wgb = wload(wb_in[:, COL_GB + fg * 512: COL_GB + (fg + 1) * 512].rearrange("(c p) n -> p c n", p=128), D_["wb_in"])
                        wbr = wload(wb_br[:, fg * 512:(fg + 1) * 512].rearrange("(c p) n -> p c n", p=128), D_["wb_br"])
                        for fb in range(4):
                            f = fg * 4 + fb
                            base = 4 * (gcnt % 2)
                            sga, sgb, tm = SG[gcnt % 2]
                            gcnt += 1
                            bga, bgb, boa, bob = base, base + 1, base + 2, base + 3
                            for c in range(16):
                                P.op("pe", lambda e_, c=c, fb=fb, wga=wga, bga=bga: e_.matmul(PB[bga].ap, lhsT=wga.ap[:, c, fb * 128:(fb + 1) * 128], rhs=HTE.ap[:, c, :], start=(c == 0), stop=(c == 15)),
                                     reads=[wga.b, HTE.b], writes=[PB[bga].b])
                            for c in range(16):
                                P.op("pe", lambda e_, c=c, fb=fb, wgb=wgb, bgb=bgb: e_.matmul(PB[bgb].ap, lhsT=wgb.ap[:, c, fb * 128:(fb + 1) * 128], rhs=HTE.ap[:, c, :], start=(c == 0), stop=(c == 15)),
                                     reads=[wgb.b, HTE.b], writes=[PB[bgb].b])
                            for c in range(4):
                                P.op("pe", lambda e_, c=c, fb=fb, wbr=wbr, boa=boa: e_.matmul(PB[boa].ap, lhsT=wbr.ap[:, c, fb * 128:(fb + 1) * 128], rhs=YTL.ap[:, c, :], start=(c == 0), stop=(c == 3)),
                                     reads=[wbr.b, YTL.b], writes=[PB[boa].b])
                            for c in range(4, 16):
                                P.op("pe", lambda e_, c=c, fb=fb, wbr=wbr, bob=bob: e_.matmul(PB[bob].ap, lhsT=wbr.ap[:, c, fb * 128:(fb + 1) * 128], rhs=YTL.ap[:, c, :], start=(c == 4), stop=(c == 15)),
                                     reads=[wbr.b, YTL.b], writes=[PB[bob].b])
                            P.op("act", lambda e_, bga=bga, sga=sga: e_.activation(out=sga.ap, in_=PB[bga].ap, func=AF.Sigmoid), reads=[PB[bga].b], writes=[sga.b])
                            P.op("act", lambda e_, bgb=bgb, sgb=sgb: e_.activation(out=sgb.ap, in_=PB[bgb].ap, func=AF.Sigmoid), reads=[PB[bgb].b], writes=[sgb.b])
                            P.op("dve", lambda e_, boa=boa, sga=sga, tm=tm: e_.tensor_tensor(out=tm.ap, in0=PB[boa].ap, in1=sga.ap, op=ALU.mult), reads=[PB[boa].b, sga.b], writes=[tm.b])
                            P.op("dve", lambda e_, bob=bob, sgb=sgb: e_.tensor_tensor(out=sgb.ap, in0=PB[bob].ap, in1=sgb.ap, op=ALU.mult), reads=[PB[bob].b, sgb.b], writes=[sgb.b])
                            P.op("pool", lambda e_, f=f, tm=tm, sgb=sgb: e_.tensor_tensor(out=MG.ap[:, f, :], in0=tm.ap, in1=sgb.ap, op=ALU.add), reads=[tm.b, sgb.b], writes=[MG.b])
                    pcnt = 0
                    for cb in range(4):
                        wo = wload(wb_out[:, cb * 512:(cb + 1) * 512].rearrange("(c p) n -> p c n", p=128), D_["wb_out"])
                        for sub in range(4):
                            bk = pcnt % 4
                            pcnt += 1
                            for c in range(16):
                                P.op("pe", lambda e_, c=c, sub=sub, wo=wo, bk=bk: e_.matmul(PB[bk].ap, lhsT=MG.ap[:, c, sub * 128:(sub + 1) * 128], rhs=wo.ap[:, c, :], start=(c == 0), stop=(c == 15)),
                                     reads=[MG.b, wo.b], writes=[PB[bk].b])
                            xs = X1[sub]
                            P.op("dve", lambda e_, bk=bk, xs=xs, cb=cb: e_.tensor_tensor(out=xs.ap[:, cb * 512:(cb + 1) * 512], in0=PB[bk].ap, in1=xs.ap[:, cb * 512:(cb + 1) * 512], op=ALU.add),
                                 reads=[PB[bk].b, xs.b], writes=[xs.b])
                    rms_front(None, X1, HBE, HTE, STE, (4, 5), "e4")
                    for hf in range(2):
                        ecnt = 0
                        for fg in range(8):
                            col0 = hf * 4096 + fg * 512
                            w1 = wload(wb_ff1[:, col0:col0 + 512].rearrange("(c p) n -> p c n", p=128), D_["wb_ff1"])
                            for fb in range(4):
                                bk = ecnt % 4
                                rl = RL[ecnt % 2]
                                ecnt += 1
                                for c in range(16):
                                    P.op("pe", lambda e_, c=c, fb=fb, w1=w1, bk=bk: e_.matmul(PB[bk].ap, lhsT=w1.ap[:, c, fb * 128:(fb + 1) * 128], rhs=HTE.ap[:, c, :], start=(c == 0), stop=(c == 15)),
                                         reads=[w1.b, HTE.b], writes=[PB[bk].b])
                                P.op("act", lambda e_, bk=bk, rl=rl: e_.activation(out=rl.ap, in_=PB[bk].ap, func=AF.Relu), reads=[PB[bk].b], writes=[rl.b])
                                ub = fg * 4 + fb
                                if ecnt % 2 == 0:
                                    P.op("dve", lambda e_, rl=rl, ub=ub: e_.tensor_tensor(out=UT.ap[:, ub, :], in0=rl.ap, in1=rl.ap, op=ALU.mult), reads=[rl.b], writes=[UT.b])
                                else:
                                    P.op("pool", lambda e_, rl=rl, ub=ub: e_.tensor_tensor(out=UT.ap[:, ub, :], in0=rl.ap, in1=rl.ap, op=ALU.mult), reads=[rl.b], writes=[UT.b])
                        for cb in range(4):
                            w2 = []
                            for kq in range(2):
                                r0 = hf * 4096 + kq * 2048
                                w2.append(wload(wb_ff2[r0:r0 + 2048, cb * 512:(cb + 1) * 512].rearrange("(c p) n -> p c n", p=128), D_["wb_ff2"]))
                            for sub in range(4):
                                bk = 4 + sub
                                for kq in range(2):
                                    for c in range(16):
                                        uc = kq * 16 + c
                                        P.op("pe", lambda e_, c=c, uc=uc, sub=sub, kq=kq, w2=w2, bk=bk: e_.matmul(PB[bk].ap, lhsT=UT.ap[:, uc, sub * 128:(sub + 1) * 128], rhs=w2[kq].ap[:, c, :], start=(uc == 0), stop=(uc == 31)),
                                             reads=[UT.b, w2[kq].b], writes=[PB[bk].b])
                                xs = X1[sub]
                                P.op("dve", lambda e_, bk=bk, xs=xs, cb=cb: e_.tensor_tensor(out=xs.ap[:, cb * 512:(cb + 1) * 512], in0=PB[bk].ap, in1=xs.ap[:, cb * 512:(cb + 1) * 512], op=ALU.add),
                                     reads=[PB[bk].b, xs.b], writes=[xs.b])
                    for sub in range(4):
                        xs = X1[sub]
                        ss, sd, rs = STE[sub]
                        outs = OUTS[0]
                        P.op("act", lambda e_, xs=xs, outs=outs, ss=ss: e_.activation(out=outs.ap, in_=xs.ap, func=AF.Square, accum_out=ss.ap), reads=[xs.b], writes=[outs.b, ss.b])
                        P.op("act", lambda e_, ss=ss, sd=sd: e_.activation(out=sd.ap, in_=ss.ap, func=AF.Sqrt, scale=1.0 / DM, bias=epsc.ap[:, 0:1]), reads=[ss.b, epsc.b], writes=[sd.b])
                        P.op("dve", lambda e_, sd=sd, rs=rs: e_.reciprocal(out=rs.ap, in_=sd.ap), reads=[sd.b], writes=[rs.b])
                        P.op("dve", lambda e_, xs=xs, outs=outs, rs=rs: e_.scalar_tensor_tensor(out=outs.ap, in0=xs.ap, scalar=rs.ap[:, 0:1], in1=GF.ap, op0=ALU.mult, op1=ALU.mult),
                             reads=[xs.b, rs.b, GF.b], writes=[outs.b])
                        r0 = s * CH + tok0 + sub * 128
                        P.dma("pool", y[r0:r0 + 128, :], outs.ap, reads=[outs.b], writes=[D_["y"]])
                P.barrier()

        P.emit_all([])
    return nc


def _const_tables():
    n = 12
    slopes = 2.0 ** (-8.0 * np.arange(1, n + 1, dtype=np.float64) / n)
    i = np.arange(128)[:, None]
    j = np.arange(128)[None, :]
    ab = np.zeros((128, 12, 2, 128), np.float32)
    for g in range(3):
        for hs in range(4):
            hd = 4 * g + hs
            for a, rel in enumerate((i - 64 - j, i + 64 - j)):
                v = -slopes[hd] * DIL[g] * np.abs(rel)
                ab[:, hd, a, :] = np.where(np.abs(rel) <= 64, v, NEG)
    rmat = np.zeros((128, 128), np.float32)
    for m in range(128):
        pr = m + 32 if (m % 64) < 32 else m - 32
        rmat[pr, m] = 1.0
    return ab.reshape(128, 12 * 256), rmat


def _rope_tables(c):
    u = np.arange(SEQ)
    t = (c * CH - 1024 + u) % SEQ
    row = (t // 64).astype(np.float32)
    col = (t % 64).astype(np.float32)
    half = 64
    inv = (10000.0 ** (-np.arange(0, half, 2, dtype=np.float32) / half)).astype(np.float32)
    ang_r = row[None, :] * inv[:, None]
    ang_c = col[None, :] * inv[:, None]
    cosT = np.concatenate([np.cos(ang_r), np.cos(ang_r), np.cos(ang_c), np.cos(ang_c)], axis=0).astype(np.float32)
    sinT = np.concatenate([-np.sin(ang_r), np.sin(ang_r), -np.sin(ang_c), np.sin(ang_c)], axis=0).astype(np.float32)
    return np.ascontiguousarray(cosT), np.ascontiguousarray(sinT)


def _kvalid(c):
    kvd = np.zeros((128, 72), np.float32)
    m = np.arange(128)
    for g in range(3):
        d = DIL[g]
        Lr = CH // d
        nper = Lr // 128 + 1
        for r in range(d):
            for i in range(nper):
                t = c * CH - 64 * d + (128 * i + m) * d + r
                kvd[:, KT_BASE[g] + r * nper + i] = np.where((t >= 0) & (t < SEQ), 0.0, NEG)
    return kvd


_NC_CACHE = {}


def kernel(x_prompt, x_sample, g_mix, w_in, g_q, g_k, w_branch, w_out, g_mlp, w_ff1, w_ff2, g_final):
    f = lambda a: np.ascontiguousarray(np.asarray(a, dtype=np.float32))
    seqs = [f(x_prompt)[0], f(x_sample)[0], f(x_sample)[1]]
    if "nc" not in _NC_CACHE:
        _NC_CACHE["nc"] = build()
    nc = _NC_CACHE["nc"]
    ab, rmat = _const_tables()
    common = {
        "w_in": f(w_in)[0], "w_branch": f(w_branch)[0], "w_out": f(w_out)[0], "w_ff1": f(w_ff1)[0], "w_ff2": f(w_ff2)[0],
        "g_mix": f(g_mix).reshape(1, DM), "g_mlp": f(g_mlp).reshape(1, DM), "g_final": f(g_final).reshape(1, DM),
        "g_q": f(g_q).reshape(1, 128), "g_k": f(g_k).reshape(1, 128), "abt": ab, "rmat": rmat,
    }
    in_maps = []
    for c in range(NCORE):
        sh = c * CH - 1024
        xr = np.concatenate([np.roll(sq, -sh, axis=0) for sq in seqs], axis=0)
        cosT, sinT = _rope_tables(c)
        m = dict(common)
        m.update({"xr": xr, "cosT": cosT, "sinT": sinT, "kval": _kvalid(c)})
        in_maps.append(m)
    res = run_bass_kernel_spmd(nc, in_maps, core_ids=list(range(NCORE)))
    outs = [np.empty((SEQ, DM), np.float32) for _ in range(3)]
    for c in range(NCORE):
        yc = res.results[c]["y"]
        for s in range(3):
            outs[s][c * CH:(c + 1) * CH] = yc[s * CH:(s + 1) * CH]
    y_prompt = outs[0][None]
    y_sample = np.stack([outs[1], outs[2]], axis=0)
    return (y_prompt, y_sample)
```
